# Optimizing a Trainium2 kernel written in Bass

```python
import jax
import jax.numpy as jnp
from jax import lax
import numpy as np

D_MODEL = 1024
BATCH = 8
SEQ = 4096
DEPTH = 4

N_MIXERS = 4
EPS = 1e-6
D_FF = 4 * D_MODEL
CONV_WIDTH = 31
POOL_WINDOWS = (2, 4, 8, 16)
N_POOL_GROUPS = len(POOL_WINDOWS)
POOL_GROUP_DIM = D_MODEL // N_POOL_GROUPS
SHORT_CONV_WIDTH = 3
RET_HEADS = 4
RET_QK_DIM = D_MODEL // RET_HEADS
RET_V_DIM = 2 * D_MODEL // RET_HEADS
RET_QK_TOTAL = RET_HEADS * RET_QK_DIM
RET_V_TOTAL = RET_HEADS * RET_V_DIM
RET_CHUNK = 128
ROPE_BASE = 10000.0

kernel_name = "hybrid_interleaved_conv_pool_shortconv_retention"


def rms_norm(x, g):
    xf = x.astype(jnp.float32)
    y = xf * lax.rsqrt(jnp.mean(xf * xf, axis=-1, keepdims=True) + EPS)
    return (y * g.astype(jnp.float32)).astype(x.dtype)


def layer_norm(x, g, b):
    xf = x.astype(jnp.float32)
    mu = jnp.mean(xf, axis=-1, keepdims=True)
    xc = xf - mu
    y = xc * lax.rsqrt(jnp.mean(xc * xc, axis=-1, keepdims=True) + EPS)
    return (y * g.astype(jnp.float32) + b.astype(jnp.float32)).astype(x.dtype)


def causal_depthwise_conv(x, w):
    width = w.shape[0]
    return lax.conv_general_dilated(
        x, w[:, None, :].astype(x.dtype), window_strides=(1,),
        padding=[(width - 1, 0)], dimension_numbers=('NWC', 'WIO', 'NWC'),
        feature_group_count=x.shape[-1])


def conformer_conv(h, w_in, b_in, w_dw, b_dw, ln_g, ln_b, w_out, b_out):
    u = h @ w_in + b_in
    a, gate = jnp.split(u, 2, axis=-1)
    u = a * jax.nn.sigmoid(gate)
    u = causal_depthwise_conv(u, w_dw) + b_dw
    u = jax.nn.silu(layer_norm(u, ln_g, ln_b))
    return u @ w_out + b_out


def multiscale_pool(h, w_group, scale):
    b, s, d = h.shape
    hg = h.reshape(b, s, N_POOL_GROUPS, POOL_GROUP_DIM)
    hf = hg.astype(jnp.float32)
    cs = jnp.cumsum(hf, axis=1)
    t = jnp.arange(1, s + 1, dtype=jnp.float32)
    means = []
    for g, win in enumerate(POOL_WINDOWS):
        csg = cs[:, :, g]
        lagged = jnp.pad(csg, ((0, 0), (win, 0), (0, 0)))[:, :s]
        count = jnp.minimum(t, float(win))[None, :, None]
        means.append((csg - lagged) / count)
    pooled = jnp.stack(means, axis=2)
    mixed = (pooled - hf).astype(h.dtype)
    y = jnp.einsum('bsgc,gce->bsge', mixed, w_group).reshape(b, s, d)
    return y * scale


def short_gated_conv(h, w_in, w_dw, w_out):
    b_gate, c_gate, v = jnp.split(h @ w_in, 3, axis=-1)
    u = causal_depthwise_conv(c_gate * v, w_dw)
    return (b_gate * u) @ w_out


def rotary(x, positions):
    half = x.shape[-1] // 2
    inv_freq = ROPE_BASE ** (-jnp.arange(half, dtype=jnp.float32) / half)
    ang = positions.astype(jnp.float32)[..., None] * inv_freq
    cos = jnp.cos(ang)[:, :, None, :]
    sin = jnp.sin(ang)[:, :, None, :]
    x1, x2 = x[..., :half], x[..., half:]
    return jnp.concatenate([x1 * cos - x2 * sin, x1 * sin + x2 * cos], axis=-1)


def retention(h, positions, w_in, w_out):
    b, s, _ = h.shape
    n_chunks = s // RET_CHUNK
    q, k, v, g = jnp.split(h @ w_in, [RET_QK_TOTAL, 2 * RET_QK_TOTAL, 2 * RET_QK_TOTAL + RET_V_TOTAL], axis=-1)
    q = rotary(q.astype(jnp.float32).reshape(b, s, RET_HEADS, RET_QK_DIM), positions)
    k = rotary(k.astype(jnp.float32).reshape(b, s, RET_HEADS, RET_QK_DIM), positions) * (RET_QK_DIM ** -0.5)
    v = v.astype(jnp.float32).reshape(b, s, RET_HEADS, RET_V_DIM)

    def to_chunks(t):
        return t.reshape(b, n_chunks, RET_CHUNK, RET_HEADS, -1).transpose(1, 0, 3, 2, 4)

    qc, kc, vc = to_chunks(q), to_chunks(k), to_chunks(v)
    log_gamma = jnp.log1p(-jnp.exp2(-5.0 - jnp.arange(RET_HEADS, dtype=jnp.float32)))
    idx = jnp.arange(RET_CHUNK, dtype=jnp.float32)
    rel = idx[:, None] - idx[None, :]
    decay_mask = jnp.where(rel >= 0, jnp.exp(log_gamma[:, None, None] * jnp.maximum(rel, 0.0)), 0.0)
    q_decay = jnp.exp(log_gamma[:, None] * (idx + 1.0))[None, :, :, None]
    k_decay = jnp.exp(log_gamma[:, None] * (RET_CHUNK - 1.0 - idx))[None, :, :, None]
    chunk_decay = jnp.exp(log_gamma * RET_CHUNK)[None, :, None, None]

    def step(state, inp):
        q_n, k_n, v_n = inp
        scores = jnp.einsum('bhcd,bhmd->bhcm', q_n, k_n) * decay_mask[None]
        intra = jnp.einsum('bhcm,bhme->bhce', scores, v_n)
        cross = jnp.einsum('bhcd,bhde->bhce', q_n * q_decay, state)
        state = state * chunk_decay + jnp.einsum('bhcd,bhce->bhde', k_n * k_decay, v_n)
        return state, intra + cross

    state0 = jnp.zeros((b, RET_HEADS, RET_QK_DIM, RET_V_DIM), jnp.float32)
    _, o = lax.scan(step, state0, (qc, kc, vc))
    o = o.transpose(1, 0, 3, 2, 4).reshape(b, s, RET_HEADS, RET_V_DIM)
    o = o * lax.rsqrt(jnp.mean(o * o, axis=-1, keepdims=True) + EPS)
    o = o.reshape(b, s, RET_V_TOTAL).astype(h.dtype)
    return (jax.nn.silu(g) * o) @ w_out


def squared_relu_mlp(h, w_up, w_down):
    return jnp.square(jax.nn.relu(h @ w_up)) @ w_down


def setup_inputs(seed: int = 0) -> dict:
    key = jax.random.key(seed)
    ks = jax.random.split(key, 24)
    d = D_MODEL
    nrm = lambda k, shape, fan_in: jax.random.normal(k, shape, jnp.float32) * (fan_in ** -0.5)
    small = lambda k, shape: 0.02 * jax.random.normal(k, shape, jnp.float32)
    x = jax.random.normal(ks[0], (BATCH, SEQ, d), jnp.float32)
    positions = jnp.broadcast_to(jnp.arange(SEQ, dtype=jnp.int32)[None, :], (BATCH, SEQ))
    norm_g = 1.0 + 0.05 * jax.random.normal(ks[1], (DEPTH, 4, d), jnp.float32)
    mlp_up = nrm(ks[2], (DEPTH, d, D_FF), d)
    mlp_down = nrm(ks[3], (DEPTH, D_FF, d), D_FF)
    conv_w_in = nrm(ks[4], (d, 2 * d), d)
    conv_b_in = small(ks[5], (2 * d,))
    conv_dw = nrm(ks[6], (CONV_WIDTH, d), CONV_WIDTH)
    conv_dw_b = small(ks[7], (d,))
    conv_ln_g = 1.0 + 0.05 * jax.random.normal(ks[8], (d,), jnp.float32)
    conv_ln_b = small(ks[9], (d,))
    conv_w_out = nrm(ks[10], (d, d), d)
    conv_b_out = small(ks[11], (d,))
    pool_w = nrm(ks[12], (N_POOL_GROUPS, POOL_GROUP_DIM, POOL_GROUP_DIM), POOL_GROUP_DIM)
    pool_scale = 1.0 + 0.1 * jax.random.normal(ks[13], (d,), jnp.float32)
    sc_w_in = nrm(ks[14], (d, 3 * d), d)
    sc_dw = nrm(ks[15], (SHORT_CONV_WIDTH, d), SHORT_CONV_WIDTH)
    sc_w_out = nrm(ks[16], (d, d), d)
    ret_w_in = nrm(ks[17], (d, 2 * RET_QK_TOTAL + 2 * RET_V_TOTAL), d)
    ret_w_out = nrm(ks[18], (RET_V_TOTAL, d), RET_V_TOTAL)
    return {"x": x, "positions": positions, "norm_g": norm_g, "mlp_up": mlp_up, "mlp_down": mlp_down,
            "conv_w_in": conv_w_in, "conv_b_in": conv_b_in, "conv_dw": conv_dw, "conv_dw_b": conv_dw_b,
            "conv_ln_g": conv_ln_g, "conv_ln_b": conv_ln_b, "conv_w_out": conv_w_out, "conv_b_out": conv_b_out,
            "pool_w": pool_w, "pool_scale": pool_scale,
            "sc_w_in": sc_w_in, "sc_dw": sc_dw, "sc_w_out": sc_w_out,
            "ret_w_in": ret_w_in, "ret_w_out": ret_w_out}


def reference(x, positions, norm_g, mlp_up, mlp_down,
              conv_w_in, conv_b_in, conv_dw, conv_dw_b, conv_ln_g, conv_ln_b, conv_w_out, conv_b_out,
              pool_w, pool_scale,
              sc_w_in, sc_dw, sc_w_out,
              ret_w_in, ret_w_out):
    h = x
    for i in range(DEPTH):
        mixer = i % N_MIXERS
        u = rms_norm(h, norm_g[i, 0])
        if mixer == 0:
            u = conformer_conv(u, conv_w_in, conv_b_in, conv_dw, conv_dw_b, conv_ln_g, conv_ln_b, conv_w_out, conv_b_out)
        elif mixer == 1:
            u = multiscale_pool(u, pool_w, pool_scale)
        elif mixer == 2:
            u = short_gated_conv(u, sc_w_in, sc_dw, sc_w_out)
        else:
            u = retention(u, positions, ret_w_in, ret_w_out)
        h = h + rms_norm(u, norm_g[i, 1])
        u = squared_relu_mlp(rms_norm(h, norm_g[i, 2]), mlp_up[i], mlp_down[i])
        h = h + rms_norm(u, norm_g[i, 3])
    return h
```

```python
import numpy as np
from contextlib import ExitStack
import concourse.bass as bass
import concourse.mybir as mybir
from concourse.bass_utils import run_bass_kernel_spmd

F32 = mybir.dt.float32
BF16 = mybir.dt.bfloat16
I32 = mybir.dt.int32
ALU = mybir.AluOpType
AF = mybir.ActivationFunctionType

D = 1024
T = 512
EPS = 1e-6
UNIT = 256
NRING = 3
TWO_PI_HI = 6.28125
TWO_PI_LO = float(2.0 * np.pi - 6.28125)

BLK = {}
_names = (["conv_in0", "conv_in1", "cdw0", "cdw1", "cdw2", "cdw3", "conv_out"] + ["up0_%d" % i for i in range(4)] + ["dn0_%d" % i for i in range(4)]
          + ["up1_%d" % i for i in range(4)] + ["dn1_%d" % i for i in range(4)]
          + ["sc_in0", "sc_in1", "sc_in2", "scdw", "sc_out"] + ["up2_%d" % i for i in range(4)] + ["dn2_%d" % i for i in range(4)]
          + ["ret_in%d" % i for i in range(6)] + ["ret_out0", "ret_out1"]
          + ["up3_%d" % i for i in range(4)] + ["dn3_%d" % i for i in range(4)] + ["pool"])
for _i, _n in enumerate(_names):
    BLK[_n] = _i
NB = len(_names)

def V_NG(l, j, c): return (l * 4 + j) * 8 + c
V_CBIN = 128
def V_CDW(k, c): return 144 + k * 8 + c
V_CDWB = 392; V_CLNG = 400; V_CLNB = 408; V_CBOUT = 416; V_PSCALE = 424
def V_SCDW(k, c): return 432 + k * 8 + c
NV = 456
C_ID = 0; C_MASK = 128; C_QDEC = 640; C_KDEC = 1152; C_WOC = 1156; C_INVF = 1220
NCST = 1221


class View:
    __slots__ = ("ap", "spans")
    def __init__(self, ap, spans):
        self.ap = ap; self.spans = spans


class Buf:
    def __init__(self, name, handle, nbytes):
        self.name = name; self.h = handle; self.nbytes = nbytes
    def f32(self, off, n):
        return View(self.h[:, off:off + n], [(self.name, off * 4, (off + n) * 4)])
    def bf(self, off, n):
        lo = off - (off % 2); hi = off + n + ((off + n) % 2)
        ap = self.h[:, lo // 2:hi // 2].bitcast(BF16)
        if lo != off or hi != off + n:
            ap = ap[:, off - lo:off - lo + n]
        return View(ap, [(self.name, off * 2, (off + n) * 2)])
    def i32(self, off, n):
        return View(self.h[:, off:off + n].bitcast(I32), [(self.name, off * 4, (off + n) * 4)])
    def f32_3d(self, nchunk, width, c0, c1, lo, hi):
        ap = self.h[:, 0:nchunk * width].rearrange("p (c n) -> p c n", c=nchunk)[:, c0:c1, lo:hi]
        return View(ap, [(self.name, (c * width + lo) * 4, (c * width + hi) * 4) for c in range(c0, c1)])
    def bf_3d(self, nchunk, width, c0, c1, lo, hi):
        ap = self.h[:, 0:nchunk * width // 2].bitcast(BF16).rearrange("p (c n) -> p c n", c=nchunk)[:, c0:c1, lo:hi]
        return View(ap, [(self.name, (c * width + lo) * 2, (c * width + hi) * 2) for c in range(c0, c1)])


class Op:
    __slots__ = ("eng", "fn", "deps", "dma", "ticket", "sem", "semval", "signal")
    def __init__(self, eng, fn, dma):
        self.eng = eng; self.fn = fn; self.dma = dma; self.deps = {}
        self.ticket = None; self.sem = None; self.semval = None; self.signal = False


class Sched:
    def __init__(self):
        self.ops = []
        self.lastw = {}
        self.readers = {}

    @staticmethod
    def _units(views):
        for v in views:
            for (b, lo, hi) in v.spans:
                for u in range(lo // UNIT, (hi - 1) // UNIT + 1):
                    yield (b, u)

    def add(self, eng, fn, reads=(), writes=(), dma=False):
        idx = len(self.ops)
        op = Op(eng, fn, dma)
        deps = op.deps
        ru = list(self._units(reads)); wu = list(self._units(writes))
        for k in ru:
            w = self.lastw.get(k)
            if w is not None:
                deps[w] = True
        for k in wu:
            w = self.lastw.get(k)
            if w is not None and w not in deps:
                deps[w] = False
            for r in self.readers.get(k, ()):
                if r not in deps:
                    deps[r] = False
        for k in ru:
            self.readers.setdefault(k, []).append(idx)
        for k in wu:
            self.lastw[k] = idx
            self.readers[k] = []
        self.ops.append(op)
        return idx

    def finalize(self, dma_sems):
        ops = self.ops
        for op in ops:
            need = {}
            for d, raw in op.deps.items():
                dop = ops[d]
                if not dop.dma and not op.dma and dop.eng == op.eng:
                    if op.eng == "pe" or not raw:
                        continue
                need[d] = raw
            op.deps = need
            for d in need:
                ops[d].signal = True
        cnt = {}
        dcnt = {}
        for op in ops:
            if op.dma:
                q = op.eng
                n = dcnt.get(q, 0); dcnt[q] = n + 1
                sems = dma_sems[q]
                op.sem = sems[n % len(sems)]
                op.semval = 16 * (n // len(sems) + 1)
            elif op.signal:
                c = cnt.get(op.eng, 0) + 1
                cnt[op.eng] = c
                op.ticket = c

    def emit(self, eng, e, eng_sems, dma_sems):
        ops = self.ops
        waited = {}
        def wait(sem, val):
            key = id(sem)
            if waited.get(key, 0) >= val:
                return
            waited[key] = val
            e.wait_ge(sem, val)
        for op in ops:
            if op.eng != eng:
                continue
            best = {}
            for d in op.deps:
                dop = ops[d]
                if dop.dma:
                    wait(dop.sem, dop.semval)
                else:
                    if best.get(dop.eng, 0) < dop.ticket:
                        best[dop.eng] = dop.ticket
            for en, tk in best.items():
                wait(eng_sems[en], tk)
            if op.dma and op.semval > 16:
                wait(op.sem, op.semval - 16)
            if op.fn is None:
                continue
            ins = op.fn(e)
            if op.dma:
                ins.then_inc(op.sem, 16)
            elif op.signal:
                ins.then_inc(eng_sems[eng], 1)


class Builder:
    def __init__(self, S, layers=(0, 1, 2, 3), mlp=True, mix=True):
        self.S = S
        self.NT = S // T
        self.layers = layers
        self.do_mlp = mlp
        self.do_mix = mix
        self.s = Sched()

    def act(self, out, in_, func, bias=None, scale=None, accum=None):
        kw = {}
        reads = [in_]
        if bias is not None:
            if isinstance(bias, View):
                kw["bias"] = bias.ap; reads.append(bias)
            else:
                kw["bias"] = self.cbias(bias); reads.append(self.cbias_view(bias))
        if scale is not None:
            if isinstance(scale, View):
                kw["scale"] = scale.ap; reads.append(scale)
            else:
                kw["scale"] = float(scale)
        writes = [out]
        if accum is not None:
            kw["accum_out"] = accum.ap; writes.append(accum)
        self.s.add("act", lambda e: e.activation(out=out.ap, in_=in_.ap, func=func, **kw), reads, writes)

    def cbias_view(self, val):
        return self.cb[val]
    def cbias(self, val):
        return self.cb[val].ap

    def tt(self, out, a, b, op, eng="dve"):
        self.s.add(eng, lambda e: e.tensor_tensor(out=out.ap, in0=a.ap, in1=b.ap, op=op), [a, b], [out])

    def ts(self, out, a, s1, op0, s2=None, op1=None, eng="dve"):
        reads = [a]
        v1 = s1.ap if isinstance(s1, View) else float(s1)
        if isinstance(s1, View): reads.append(s1)
        v2 = None
        if s2 is not None:
            v2 = s2.ap if isinstance(s2, View) else float(s2)
            if isinstance(s2, View): reads.append(s2)
        if op1 is None:
            self.s.add(eng, lambda e: e.tensor_scalar(out=out.ap, in0=a.ap, scalar1=v1, scalar2=None, op0=op0), reads, [out])
        else:
            self.s.add(eng, lambda e: e.tensor_scalar(out=out.ap, in0=a.ap, scalar1=v1, scalar2=v2, op0=op0, op1=op1), reads, [out])

    def stt(self, out, a, sc, b, op0, op1):
        reads = [a, b]
        v = sc.ap if isinstance(sc, View) else float(sc)
        if isinstance(sc, View): reads.append(sc)
        self.s.add("dve", lambda e: e.scalar_tensor_tensor(out=out.ap, in0=a.ap, scalar=v, in1=b.ap, op0=op0, op1=op1), reads, [out])

    def copy(self, out, a, eng="dve"):
        if eng == "act":
            self.s.add("act", lambda e: e.copy(out=out.ap, in_=a.ap), [a], [out])
        else:
            self.s.add(eng, lambda e: e.tensor_copy(out=out.ap, in_=a.ap), [a], [out])

    def memset(self, out, val, eng="dve"):
        self.s.add(eng, lambda e: e.memset(out.ap, val), [], [out])

    def mm(self, out, lhsT, rhs, start, stop):
        self.s.add("pe", lambda e: e.matmul(out.ap, lhsT=lhsT.ap, rhs=rhs.ap, start=start, stop=stop), [lhsT, rhs], [out])

    def tr(self, out, in_):
        ident = self.ident
        self.s.add("pe", lambda e: e.transpose(out.ap, in_.ap, ident.ap), [in_, ident], [out])

    def dma(self, q, out, in_):
        self.s.add(q, lambda e: e.dma_start(out=out.ap, in_=in_.ap), [in_], [out], dma=True)

    def mmbank(self):
        b = self.psb[self._mmi % 6]; self._mmi += 1
        return b
    def stbank(self):
        b = self.psb[6 + self._sti % 2]; self._sti += 1
        return b
    def tmp(self):
        i = self._tmpi % 4; self._tmpi += 1
        return self.tmpf.f32(i * 544, 512), i

    def vec(self, col):
        return self.vecs.f32(col, 1)

    def wget(self, name):
        i = self._wpos
        assert self.worder[i] == BLK[name], (name, i)
        while self._wissued < min(len(self.worder), i + NRING):
            k = self._wissued
            blk = self.worder[k]
            slot = self.ring.bf((k % NRING) * 8192, 8192)
            wkey = View(self.wbf[blk], [("wbf%d" % blk, 0, 1)])
            if k < self._ntile_blocks:
                self.dma("pool", slot, View(self._cast_src[blk], [("wblk", 0, 1)]))
                if self.NT > 1:
                    self.dma("sp", wkey, slot)
            else:
                self.dma("sp", slot, wkey)
            self._wissued += 1
        self._wpos += 1
        return (i % NRING) * 8192

    def wv(self, base, off, n):
        return self.ring.bf(base + off, n)

    def stats(self, src_bf):
        ps = self.stbank().f32(0, 512)
        for c in range(8):
            self.mm(ps, self.onesw, src_bf.bf(c * 512, 512), c == 0, c == 7)
        return ps

    def rsqrt_from(self, out, ps):
        self.act(out, ps, AF.Ln, bias=EPS)
        self.act(out, out, AF.Exp, scale=-0.5)

    def prenorm(self, l, j, to_big=False):
        self.act(self.sqb.bf(0, 4096), self.h.f32(0, 4096), AF.Square)
        ps = self.stats(self.sqb)
        self.rsqrt_from(self.rstd, ps)
        for c in range(8):
            if to_big:
                out = self.big.f32(c * 544 + 32, 512)
            else:
                out = self.xn.bf(c * 512, 512)
            self.stt(out, self.h.f32(c * 512, 512), self.vec(V_NG(l, j, c)), self.rstd, ALU.mult, ALU.mult)

    def postnorm(self, l, j, final=False):
        ps = self.stats(self.sqb)
        self.rsqrt_from(self.rstd, ps)
        for c in range(8):
            uc = self.u.f32(c * 512, 512)
            self.stt(uc, uc, self.vec(V_NG(l, j, c)), self.rstd, ALU.mult, ALU.mult)
            hc = self.h.f32(c * 512, 512)
            self.tt(uc if final else hc, hc, uc, ALU.add)

    def evac_u(self, oc, ps, bias=None, scale=None):
        uo = self.u.f32(oc * 512, 512)
        if bias is not None:
            self.ts(uo, ps, bias, ALU.add)
            self.act(self.sqb.bf(oc * 512, 512), uo, AF.Square)
        elif scale is not None:
            self.act(uo, ps, AF.Identity, scale=scale)
            self.act(self.sqb.bf(oc * 512, 512), ps, AF.Square, scale=scale)
        else:
            self.act(uo, ps, AF.Copy)
            self.act(self.sqb.bf(oc * 512, 512), ps, AF.Square)

    def proj_block(self, wb, rhs_buf, evac_fn, defer=None, kouter=False):
        first = 0
        if kouter:
            banks = [self.mmbank().f32(0, 512) for _ in range(6)]
            for kc in range(8):
                for oc in range(6):
                    self.mm(banks[oc], self.wv(wb, kc * 1024 + oc * 128, 128), rhs_buf.bf(kc * 512, 512), kc == 0, kc == 7)
            if defer is not None and not defer["done"]:
                defer["fn"](); defer["done"] = True
            for oc in range(6):
                evac_fn(oc, banks[oc])
            first = 6
        pend = []
        for oc in range(first, 8):
            ps = self.mmbank().f32(0, 512)
            for kc in range(8):
                self.mm(ps, self.wv(wb, kc * 1024 + oc * 128, 128), rhs_buf.bf(kc * 512, 512), kc == 0, kc == 7)
            if defer is not None and not defer["done"]:
                pend.append((oc, ps))
                if oc == defer["after"]:
                    defer["fn"](); defer["done"] = True
                    for (o, p) in pend: evac_fn(o, p)
                    pend = []
            else:
                evac_fn(oc, ps)

    def mlp(self, l, final=False, hook=None):
        for c in range(8):
            self.act(self.xn.bf(c * 512, 512), self.h.f32(c * 512, 512), AF.Identity, scale=self.vec(V_NG(l, 2, c)))
        r2 = self.st1
        for b in range(4):
            wb = self.wget("up%d_%d" % (l, b))
            def evac_relu2(oc, ps, b=b):
                t, _ = self.tmp()
                self.act(t, ps, AF.Relu)
                self.tt(self.mid.bf((b * 8 + oc) * 512, 512), t, t, ALU.mult)
            self.proj_block(wb, self.xn, evac_relu2, kouter=(b == 0))
            if b == 0:
                self.act(self.sqb.bf(0, 4096), self.h.f32(0, 4096), AF.Square)
                ps = self.stats(self.sqb)
                self.act(r2, ps, AF.Ln, bias=EPS)
                self.act(r2, r2, AF.Exp, scale=-1.0)
            if b == 1 and hook is not None:
                hook()
        for db in range(4):
            wb = self.wget("dn%d_%d" % (l, db))
            for o2 in range(2):
                oc = db * 2 + o2
                ps = self.mmbank().f32(0, 512)
                for kc in range(32):
                    self.mm(ps, self.wv(wb, kc * 256 + o2 * 128, 128), self.mid.bf(kc * 512, 512), kc == 0, kc == 31)
                uo = self.u.f32(oc * 512, 512)
                self.tt(uo, ps, r2, ALU.mult)
                self.act(self.sqb.bf(oc * 512, 512), uo, AF.Square)
        self.postnorm(l, 3, final=final)

    def conformer(self, ti):
        GB = 8192
        for c in range(8):
            self.act(self.xn.bf(c * 512, 512), self.h.f32(c * 512, 512), AF.Identity, scale=self.vec(V_NG(0, 0, c)))
        self.act(self.sqb.bf(0, 4096), self.h.f32(0, 4096), AF.Square)
        def stats_chain():
            ps = self.stats(self.sqb)
            self.rsqrt_from(self.rstd, ps)
        self.copy(View(
            self.mid.h[:, GB // 2:(GB + 8 * 544) // 2].bitcast(BF16).rearrange("p (c n) -> p c n", c=8)[:, :, 2:32],
            [("mid", (GB + c * 544 + 2) * 2, (GB + c * 544 + 32) * 2) for c in range(8)]),
            self.halo0.bf_3d(8, 30, 0, 8, 0, 30), eng="pool")
        wb = self.wget("conv_in0")
        def evac_a(oc, ps):
            t, _ = self.tmp()
            self.tt(t, ps, self.rstd, ALU.mult)
            self.act(self.big.f32(oc * 544 + 32, 512), t, AF.Identity, bias=self.vec(V_CBIN + oc))
        self.proj_block(wb, self.xn, evac_a, {"after": 2, "fn": stats_chain, "done": False}, kouter=True)
        wb = self.wget("conv_in1")
        for oc in range(8):
            ps = self.mmbank().f32(0, 512)
            for kc in range(8):
                self.mm(ps, self.wv(wb, kc * 1024 + oc * 128, 128), self.xn.bf(kc * 512, 512), kc == 0, kc == 7)
            t, _ = self.tmp()
            self.tt(t, ps, self.rstd, ALU.mult)
            self.act(t, t, AF.Sigmoid, bias=self.vec(V_CBIN + 8 + oc))
            self.tt(self.mid.bf(GB + oc * 544 + 32, 512), self.big.f32(oc * 544 + 32, 512), t, ALU.mult)
        self.copy(self.halo0.bf_3d(8, 30, 0, 8, 0, 30), View(
            self.mid.h[:, GB // 2:(GB + 8 * 544) // 2].bitcast(BF16).rearrange("p (c n) -> p c n", c=8)[:, :, 514:544],
            [("mid", (GB + c * 544 + 514) * 2, (GB + c * 544 + 544) * 2) for c in range(8)]), eng="pool")
        for b in range(4):
            wb = self.wget("cdw%d" % b)
            for cc in range(2):
                c = 2 * b + cc
                ps = self.mmbank().f32(0, 512)
                for k in range(31):
                    self.mm(ps, self.wv(wb, (cc * 31 + k) * 128, 128), self.mid.bf(GB + c * 544 + 2 + k, 512), k == 0, k == 30)
                uc = self.u.f32(c * 512, 512)
                self.ts(uc, ps, self.vec(V_CDWB + c), ALU.add)
                self.act(self.xn.bf(c * 512, 512), uc, AF.Copy)
                self.act(self.sqb.bf(c * 512, 512), uc, AF.Square)
        psm = self.stats(self.xn)
        pss = self.stats(self.sqb)
        self.act(self.st1, psm, AF.Copy)
        t, _ = self.tmp()
        self.stt(t, self.st1, -1.0, self.st1, ALU.mult, ALU.mult)
        self.tt(t, t, pss, ALU.add)
        self.rsqrt_from(self.rstd, t)
        for c in range(8):
            t, _ = self.tmp()
            self.tt(t, self.u.f32(c * 512, 512), self.st1, ALU.subtract)
            self.stt(t, t, self.vec(V_CLNG + c), self.rstd, ALU.mult, ALU.mult)
            self.act(self.mid.bf(c * 512, 512), t, AF.Silu, bias=self.vec(V_CLNB + c))
        wb = self.wget("conv_out")
        for oc in range(8):
            ps = self.mmbank().f32(0, 512)
            for kc in range(8):
                self.mm(ps, self.wv(wb, kc * 1024 + oc * 128, 128), self.mid.bf(kc * 512, 512), kc == 0, kc == 7)
            self.evac_u(oc, ps, bias=self.vec(V_CBOUT + oc))
        self.postnorm(0, 1)

    def poolmix(self, ti):
        self.prenorm(1, 0, to_big=True)
        self.copy(self.big.f32_3d(8, 544, 0, 8, 17, 32), self.halo1.f32_3d(8, 15, 0, 8, 0, 15), eng="pool")
        self.copy(self.halo1.f32_3d(8, 15, 0, 8, 0, 15), self.big.f32_3d(8, 544, 0, 8, 529, 544), eng="pool")
        for g in range(4):
            win = 2 << g
            for cc in range(2):
                c = 2 * g + cc
                cur_buf, cur_off = self.big, c * 544
                for i in range(1, g + 2):
                    sh = 1 << (i - 1)
                    lo = 16 + (1 << i)
                    _, si = self.tmp()
                    dst_off = si * 544
                    self.tt(self.tmpf.f32(dst_off + lo, 544 - lo), cur_buf.f32(cur_off + lo, 544 - lo),
                            cur_buf.f32(cur_off + lo - sh, 544 - lo), ALU.add, eng=("pool" if cc == 1 else "dve"))
                    cur_buf, cur_off = self.tmpf, dst_off
                if ti == 0:
                    fx = cur_buf.f32(cur_off + 32, 16)
                    self.tt(fx, fx, self.cst.f32(C_WOC + g * 16, 16), ALU.mult)
                self.stt(self.xn.bf(c * 512, 512), cur_buf.f32(cur_off + 32, 512), 1.0 / win,
                         self.big.f32(c * 544 + 32, 512), ALU.mult, ALU.subtract)
        for g in range(4):
            for o2 in range(2):
                c = 2 * g + o2
                ps = self.mmbank().f32(0, 512)
                for kc in range(2):
                    self.mm(ps, self.pw.bf((g * 2 + kc) * 256 + o2 * 128, 128), self.xn.bf((2 * g + kc) * 512, 512), kc == 0, kc == 1)
                self.evac_u(c, ps, scale=self.vec(V_PSCALE + c))
        self.postnorm(1, 1)

    def shortconv(self, ti):
        GB = 8192
        def gbv(lo, hi):
            return View(self.mid.h[:, GB // 2:(GB + 8 * 544) // 2].bitcast(BF16).rearrange("p (c n) -> p c n", c=8)[:, :, lo:hi],
                        [("mid", (GB + c * 544 + lo) * 2, (GB + c * 544 + hi) * 2) for c in range(8)])
        for c in range(8):
            self.act(self.xn.bf(c * 512, 512), self.h.f32(c * 512, 512), AF.Identity, scale=self.vec(V_NG(2, 0, c)))
        self.act(self.sqb.bf(0, 4096), self.h.f32(0, 4096), AF.Square)
        r2 = self.st1
        def stats_chain():
            ps = self.stats(self.sqb)
            self.act(self.rstd, ps, AF.Ln, bias=EPS)
            self.act(r2, self.rstd, AF.Exp, scale=-1.0)
            self.act(self.rstd, self.rstd, AF.Exp, scale=-0.5)
        self.copy(gbv(30, 32), self.halo2.bf_3d(8, 2, 0, 8, 0, 2), eng="pool")
        defer = {"after": 2, "fn": stats_chain, "done": False}
        wb = self.wget("sc_in0")
        self.proj_block(wb, self.xn, lambda oc, ps: self.tt(self.u.f32(oc * 512, 512), ps, self.rstd, ALU.mult), defer, kouter=True)
        wb = self.wget("sc_in1")
        self.proj_block(wb, self.xn, lambda oc, ps: self.tt(self.big.f32(oc * 544 + 32, 512), ps, r2, ALU.mult))
        wb = self.wget("sc_in2")
        self.proj_block(wb, self.xn, lambda oc, ps: self.tt(self.mid.bf(GB + oc * 544 + 32, 512), self.big.f32(oc * 544 + 32, 512), ps, ALU.mult))
        self.copy(self.halo2.bf_3d(8, 2, 0, 8, 0, 2), gbv(542, 544), eng="pool")
        wb = self.wget("scdw")
        for c in range(8):
            ps = self.mmbank().f32(0, 512)
            for k in range(3):
                self.mm(ps, self.wv(wb, (c * 3 + k) * 128, 128), self.mid.bf(GB + c * 544 + 30 + k, 512), k == 0, k == 2)
            self.tt(self.mid.bf(c * 512, 512), self.u.f32(c * 512, 512), ps, ALU.mult)
        wb = self.wget("sc_out")
        self.proj_block(wb, self.mid, lambda oc, ps: self.evac_u(oc, ps))
        self.postnorm(2, 1)

    def sincos(self, ti):
        t0 = ti * T
        pi_, _ = self.tmp()
        posi = View(pi_.ap.bitcast(I32), pi_.spans)
        src = View(self.pos[0:1, t0:t0 + T].partition_broadcast(128), [("pos", 0, 1)])
        self.dma("sp", posi, src)
        ang, _ = self.tmp()
        self.copy(ang, posi)
        self.ts(ang, ang, self.cst.f32(C_INVF, 1), ALU.mult)
        for which in range(2):
            a2, _ = self.tmp()
            if which == 0:
                self.ts(a2, ang, 1.0, ALU.mult)
            else:
                self.ts(a2, ang, float(np.pi / 2), ALU.add)
            kk, _ = self.tmp()
            ki = View(kk.ap.bitcast(I32), kk.spans)
            self.ts(ki, a2, float(1.0 / (2 * np.pi)), ALU.mult)
            dst = self.cs.f32(which * 512, 512)
            self.copy(dst, ki)
            self.stt(a2, dst, -TWO_PI_HI, a2, ALU.mult, ALU.add)
            self.stt(a2, dst, -TWO_PI_LO, a2, ALU.mult, ALU.add)
            self.ts(a2, a2, -3.1415925, ALU.max, 3.1415925, ALU.min)
            self.act(dst, a2, AF.Sin)

    def rotary_block(self, wb, dst_off, pre=None):
        sin = self.cs.f32(0, 512); cos = self.cs.f32(512, 512)
        banks = {}
        if pre is not None:
            for oc in range(6):
                banks[oc] = self.mmbank().f32(0, 512)
            for kc in range(8):
                for oc in range(6):
                    self.mm(banks[oc], self.wv(wb, kc * 1024 + oc * 128, 128), self.xn.bf(kc * 512, 512), kc == 0, kc == 7)
            pre()
        for hh in range(4):
            pss = []
            for half in range(2):
                oc = 2 * hh + half
                if oc in banks:
                    pss.append(banks[oc]); continue
                ps = self.mmbank().f32(0, 512)
                for kc in range(8):
                    self.mm(ps, self.wv(wb, kc * 1024 + oc * 128, 128), self.xn.bf(kc * 512, 512), kc == 0, kc == 7)
                pss.append(ps)
            t1, _ = self.tmp(); t2, _ = self.tmp(); t3, _ = self.tmp(); t4, _ = self.tmp()
            self.tt(t1, pss[0], cos, ALU.mult)
            self.tt(t2, pss[1], sin, ALU.mult)
            self.tt(t3, pss[0], sin, ALU.mult)
            self.tt(t4, pss[1], cos, ALU.mult)
            self.tt(self.mid.bf(dst_off + (2 * hh) * 512, 512), t1, t2, ALU.subtract, eng="pool")
            self.tt(self.mid.bf(dst_off + (2 * hh + 1) * 512, 512), t3, t4, ALU.add, eng="pool")

    def retention(self, ti):
        QR, QD, KR, KT = 0, 4096, 8192, 12288
        if not (self.do_mlp and 2 in self.layers):
            self.sincos(ti)
        for c in range(8):
            self.act(self.xn.bf(c * 512, 512), self.h.f32(c * 512, 512), AF.Identity, scale=self.vec(V_NG(3, 0, c)))
        self.act(self.sqb.bf(0, 4096), self.h.f32(0, 4096), AF.Square)
        rT = self.misc.f32(1600, 4)
        def stats_chain():
            ps = self.stats(self.sqb)
            self.rsqrt_from(self.rstd, ps)
            for w in range(2):
                csw = self.cs.f32(w * 512, 512)
                self.tt(csw, csw, self.rstd, ALU.mult)
            pT = self.stbank()
            for j in range(4):
                for kc in range(8):
                    self.mm(pT.f32(j, 1), self.sqb.bf(kc * 512 + j * 128, 128), self.misc.bf(2944, 1), kc == 0, kc == 7)
            self.act(rT, pT.f32(0, 4), AF.Ln, bias=EPS)
            self.act(rT, rT, AF.Exp, scale=-0.5)
        wb = self.wget("ret_in0")
        self.rotary_block(wb, QR, pre=stats_chain)
        for c in range(8):
            hh = c // 2
            for j in range(4):
                self.tt(self.mid.bf(QD + c * 512 + j * 128, 128), self.mid.bf(QR + c * 512 + j * 128, 128),
                        self.cst.f32(C_QDEC + hh * 128, 128), ALU.mult, eng="pool")
        wb = self.wget("ret_in1")
        self.rotary_block(wb, KR)
        for j in range(4):
            for hh in range(4):
                psb = self.mmbank()
                for dc in range(2):
                    self.tr(psb.bf(dc * 128, 128), self.mid.bf(KR + (2 * hh + dc) * 512 + j * 128, 128))
                self.act(self.mid.bf(KT + (j * 4 + hh) * 256, 256), psb.bf(0, 256), AF.Identity,
                         scale=self.cst.f32(C_KDEC + hh, 1))
        for b in range(2):
            wb = self.wget("ret_in%d" % (2 + b))
            for j in range(4):
                for h2 in range(2):
                    hh = 2 * b + h2
                    ps = self.mmbank().f32(0, 512)
                    for kc in range(8):
                        self.mm(ps, self.xn.bf(kc * 512 + j * 128, 128), self.wv(wb, kc * 1024 + h2 * 512, 512), kc == 0, kc == 7)
                    self.act(self.u.bf((j * 4 + hh) * 512, 512), ps, AF.Identity, scale=self.misc.f32(1600 + j, 1))
        for j in range(4):
            ogoff = (j % 2) * 2048
            scs = []
            for hh in range(4):
                pS = self.mmbank().f32(0, 128)
                for dc in range(2):
                    self.mm(pS, self.mid.bf(KR + (2 * hh + dc) * 512 + j * 128, 128),
                            self.mid.bf(QR + (2 * hh + dc) * 512 + j * 128, 128), dc == 0, dc == 1)
                sc = self.scb.bf(hh * 128, 128)
                self.tt(sc, pS, self.cst.f32(C_MASK + hh * 128, 128), ALU.mult)
                scs.append(sc)
            pOs = []
            for hh in range(4):
                vt = self.u.bf((j * 4 + hh) * 512, 512)
                pO = self.mmbank().f32(0, 512)
                self.mm(pO, scs[hh], vt, True, False)
                for dc in range(2):
                    self.mm(pO, self.mid.bf(QD + (2 * hh + dc) * 512 + j * 128, 128),
                            self.sbf.bf((hh * 2 + dc) * 512, 512), False, dc == 1)
                ss = self.misc.f32(1152 + (self._ssi % 4) * 64, 1); self._ssi += 1
                junk, _ = self.tmp()
                self.act(junk, pO, AF.Square, accum=ss)
                self.act(ss, ss, AF.Ln, bias=EPS, scale=1.0 / 512)
                self.act(ss, ss, AF.Exp, scale=-0.5)
                self.ts(self.sqb.bf(ogoff + hh * 512, 512), pO, ss, ALU.mult)
            for hh in range(4):
                vt = self.u.bf((j * 4 + hh) * 512, 512)
                for dc in range(2):
                    pD = self.mmbank().f32(0, 512)
                    self.mm(pD, self.mid.bf(KT + (j * 4 + hh) * 256 + dc * 128, 128), vt, True, True)
                    st = self.st.f32((hh * 2 + dc) * 512, 512)
                    self.stt(st, st, float(self.g128[hh]), pD, ALU.mult, ALU.add)
                    self.copy(self.sbf.bf((hh * 2 + dc) * 512, 512), st, eng="pool")
            for half in range(2):
                psb = self.mmbank()
                for f in range(8):
                    fc = half * 8 + f
                    self.tr(psb.bf(f * 128, 128), self.sqb.bf(ogoff + fc * 128, 128))
                self.copy(self.big.bf_3d(16, 512, half * 8, half * 8 + 8, j * 128, j * 128 + 128),
                          View(psb.bf(0, 1024).ap.rearrange("p (c n) -> p c n", c=8), psb.bf(0, 1024).spans), eng="dve")
        for b in range(2):
            wb = self.wget("ret_in%d" % (4 + b))
            for oc in range(8):
                fc = b * 8 + oc
                ps = self.mmbank().f32(0, 512)
                for kc in range(8):
                    self.mm(ps, self.wv(wb, kc * 1024 + oc * 128, 128), self.xn.bf(kc * 512, 512), kc == 0, kc == 7)
                t, _ = self.tmp()
                self.tt(t, ps, self.rstd, ALU.mult)
                self.act(t, t, AF.Silu)
                o = self.big.bf(fc * 512, 512)
                self.tt(o, o, t, ALU.mult)
        for b in range(2):
            wb = self.wget("ret_out%d" % b)
            for o4 in range(4):
                oc = b * 4 + o4
                ps = self.mmbank().f32(0, 512)
                for kc in range(16):
                    self.mm(ps, self.wv(wb, kc * 512 + o4 * 128, 128), self.big.bf(kc * 512, 512), kc == 0, kc == 15)
                self.evac_u(oc, ps)
        self.postnorm(3, 1)

    def build(self):
        S = self.S
        nc = bass.Bass("TRN2", target_bir_lowering=False)
        self.nc = nc
        xT = nc.dram_tensor("xT", [D, S], F32, kind="ExternalInput").ap()
        outT = nc.dram_tensor("outT", [D, S], F32, kind="ExternalOutput").ap()
        wblk = nc.dram_tensor("wblk", [NB, 128, 8192], F32, kind="ExternalInput").ap()
        self.wbf = nc.dram_tensor("wbf", [NB, 128, 8192], BF16, kind="Internal").ap()
        vecs_d = nc.dram_tensor("vecs", [128, NV], F32, kind="ExternalInput").ap()
        cst_d = nc.dram_tensor("cst", [128, NCST], F32, kind="ExternalInput").ap()
        self.pos = nc.dram_tensor("pos", [1, S], I32, kind="ExternalInput").ap()
        lg = np.log1p(-np.exp2(-5.0 - np.arange(4, dtype=np.float64)))
        self.g128 = np.exp(lg * 128.0)

        order_tile = []
        for l in range(4):
            if l in self.layers:
                if l == 0 and self.do_mix: order_tile += ["conv_in0", "conv_in1", "cdw0", "cdw1", "cdw2", "cdw3", "conv_out"]
                if l == 2 and self.do_mix: order_tile += ["sc_in0", "sc_in1", "sc_in2", "scdw", "sc_out"]
                if l == 3 and self.do_mix: order_tile += ["ret_in%d" % i for i in range(6)] + ["ret_out0", "ret_out1"]
                if self.do_mlp:
                    order_tile += ["up%d_%d" % (l, i) for i in range(4)] + ["dn%d_%d" % (l, i) for i in range(4)]
        self.worder = [BLK[n] for n in order_tile] * self.NT
        self._wpos = 0; self._wissued = 0
        self._mmi = 0; self._sti = 0; self._tmpi = 0; self._sci = 0; self._ssi = 0

        with ExitStack() as es:
            def sb(name, nbytes):
                hnd = es.enter_context(nc.sbuf_tensor(name, [128, nbytes // 4], F32))
                return Buf(name, hnd, nbytes)
            self.h = sb("h", 16384)
            self.xn = sb("xn", 8192)
            self.mid = sb("mid", 32768)
            self.u = sb("u", 16384)
            self.sqb = sb("sqb", 8192)
            self.big = sb("big", 8 * 544 * 4)
            self.tmpf = sb("tmpf", 4 * 544 * 4)
            self.ring = sb("ring", NRING * 16384)
            self.st = sb("st", 16384)
            self.sbf = sb("sbf", 8192)
            self.vecs = sb("vecs_sb", NV * 4)
            self.cst = sb("cst_sb", NCST * 4)
            self.pw = sb("pw", 4096)
            self.cs = sb("cs", 4096)
            self.halo0 = sb("halo0", 8 * 30 * 2)
            self.halo1 = sb("halo1", 8 * 15 * 4)
            self.halo2 = sb("halo2", 64)
            self.scb = sb("scb", 1024)
            misc = sb("misc", 6656)
            self.misc = misc
            self.rstd = misc.f32(0, 512)
            self.st1 = misc.f32(512, 512)
            self.onesw = misc.bf(2944, 128)
            self.ident = misc.bf(3072, 128)
            self.cb = {EPS: misc.f32(1408, 1)}
            self.psb = []
            for i in range(8):
                hnd = es.enter_context(nc.psum_tensor("ps%d" % i, [128, 512], F32))
                self.psb.append(Buf("ps%d" % i, hnd, 2048))
            eng_sems = {}
            for en in ("pe", "act", "dve", "pool", "sp"):
                eng_sems[en] = es.enter_context(nc.semaphore("sem_" + en))
            dma_sems = {"sp": [es.enter_context(nc.semaphore("dsp%d" % i)) for i in range(12)],
                        "pool": [es.enter_context(nc.semaphore("dpl%d" % i)) for i in range(16)]}

            self.dma("sp", self.vecs.f32(0, NV), View(vecs_d[:, :], [("vecs_d", 0, 1)]))
            self.dma("sp", self.cst.f32(0, NCST), View(cst_d[:, :], [("cst_d", 0, 1)]))
            self.memset(self.onesw, 1.0 / 1024)
            self.memset(self.cb[EPS], EPS)
            self.copy(self.ident, self.cst.f32(C_ID, 128))
            self.memset(self.st.f32(0, 4096), 0.0)
            self.memset(self.sbf.bf(0, 4096), 0.0)
            self.memset(self.halo0.f32(0, 120), 0.0)
            self.memset(self.halo1.f32(0, 120), 0.0)
            self.memset(self.halo2.f32(0, 16), 0.0)
            self.memset(self.big.f32(0, 8 * 544), 0.0)
            self._cast_src = wblk
            self._ntile_blocks = len(order_tile)
            if 1 in self.layers and self.do_mix:
                self.dma("pool", self.pw.bf(0, 2048), View(wblk[BLK["pool"]][:, 0:2048], [("wblk", 0, 1)]))

            last_l = max(self.layers) if self.layers else -1
            for ti in range(self.NT):
                t0 = ti * T
                for c in range(8):
                    self.dma("sp", self.h.f32(c * 512, 512), View(xT[c * 128:(c + 1) * 128, t0:t0 + T], [("xT", 0, 1)]))
                for l in range(4):
                    if l not in self.layers:
                        continue
                    if not self.do_mix: pass
                    elif l == 0: self.conformer(ti)
                    elif l == 1: self.poolmix(ti)
                    elif l == 2: self.shortconv(ti)
                    else: self.retention(ti)
                    if self.do_mlp:
                        hk = None
                        if l == 2 and 3 in self.layers and self.do_mix:
                            hk = (lambda ti=ti: self.sincos(ti))
                        self.mlp(l, final=(l == last_l), hook=hk)
                src = self.u if (self.do_mlp and last_l in self.layers) else self.h
                for c in range(8):
                    self.dma("pool", View(outT[c * 128:(c + 1) * 128, t0:t0 + T], [("outT%d_%d" % (ti, c), 0, 1)]), src.f32(c * 512, 512))
            self.s.add("pool", None, reads=[View(None, [("outT%d_%d" % (ti, c), 0, 1) for ti in range(self.NT) for c in range(8)])], writes=[])

            self.s.finalize(dma_sems)
            sched = self.s
            with nc.Block() as block:
                @block.sync
                def _(e): sched.emit("sp", e, eng_sems, dma_sems)
                @block.tensor
                def _(e): sched.emit("pe", e, eng_sems, dma_sems)
                @block.scalar
                def _(e): sched.emit("act", e, eng_sems, dma_sems)
                @block.vector
                def _(e): sched.emit("dve", e, eng_sems, dma_sems)
                @block.gpsimd
                def _(e): sched.emit("pool", e, eng_sems, dma_sems)
        return nc


def _kblock(W, col0, ncols):
    K = W.shape[0]
    sub = W[:, col0:col0 + ncols].reshape(K // 128, 128, ncols)
    out = np.ascontiguousarray(sub.transpose(1, 0, 2)).reshape(128, (K // 128) * ncols)
    if out.shape[1] < 8192:
        out = np.concatenate([out, np.zeros((128, 8192 - out.shape[1]), np.float32)], axis=1)
    return out


def _pack_weights(inp):
    blocks = [None] * NB
    for i in range(2): blocks[BLK["conv_in%d" % i]] = _kblock(inp["conv_w_in"], i * 1024, 1024)
    blocks[BLK["conv_out"]] = _kblock(inp["conv_w_out"], 0, 1024)
    dw = inp["conv_dw"]
    pidx = np.arange(128)
    for b in range(4):
        blk = np.zeros((128, 8192), np.float32)
        for cc in range(2):
            c = 2 * b + cc
            for k in range(31):
                d = cc * 31 + k
                blk[pidx, d * 128 + pidx] = dw[k, c * 128:(c + 1) * 128]
        blocks[BLK["cdw%d" % b]] = blk
    for i in range(3): blocks[BLK["sc_in%d" % i]] = _kblock(inp["sc_w_in"], i * 1024, 1024)
    blocks[BLK["sc_out"]] = _kblock(inp["sc_w_out"], 0, 1024)
    blk = np.zeros((128, 8192), np.float32)
    for c in range(8):
        for k in range(3):
            blk[np.arange(128), (c * 3 + k) * 128 + np.arange(128)] = inp["sc_dw"][k, c * 128:(c + 1) * 128]
    blocks[BLK["scdw"]] = blk
    for i in range(6): blocks[BLK["ret_in%d" % i]] = _kblock(inp["ret_w_in"], i * 1024, 1024)
    for i in range(2): blocks[BLK["ret_out%d" % i]] = _kblock(inp["ret_w_out"], i * 512, 512)
    for l in range(4):
        for i in range(4):
            blocks[BLK["up%d_%d" % (l, i)]] = _kblock(inp["mlp_up"][l], i * 1024, 1024)
            blocks[BLK["dn%d_%d" % (l, i)]] = _kblock(inp["mlp_down"][l], i * 256, 256)
    pw = inp["pool_w"]
    pb = np.ascontiguousarray(pw.reshape(4, 2, 128, 256).transpose(2, 0, 1, 3)).reshape(128, 2048)
    blocks[BLK["pool"]] = np.concatenate([pb, np.zeros((128, 8192 - 2048), np.float32)], axis=1)
    return np.ascontiguousarray(np.stack(blocks, 0).astype(np.float32))


def _col(v):
    return np.ascontiguousarray(np.asarray(v, np.float32).reshape(-1, 128).T)


def _pack_vecs(inp):
    vecs = np.zeros((128, NV), np.float32)
    ng = inp["norm_g"]
    for l in range(4):
        for j in range(4):
            vecs[:, V_NG(l, j, 0):V_NG(l, j, 0) + 8] = _col(ng[l, j])
    vecs[:, V_CBIN:V_CBIN + 16] = _col(inp["conv_b_in"])
    for k in range(31):
        vecs[:, V_CDW(k, 0):V_CDW(k, 0) + 8] = _col(inp["conv_dw"][k])
    vecs[:, V_CDWB:V_CDWB + 8] = _col(inp["conv_dw_b"])
    vecs[:, V_CLNG:V_CLNG + 8] = _col(inp["conv_ln_g"])
    vecs[:, V_CLNB:V_CLNB + 8] = _col(inp["conv_ln_b"])
    vecs[:, V_CBOUT:V_CBOUT + 8] = _col(inp["conv_b_out"])
    vecs[:, V_PSCALE:V_PSCALE + 8] = _col(inp["pool_scale"])
    for k in range(3):
        vecs[:, V_SCDW(k, 0):V_SCDW(k, 0) + 8] = _col(inp["sc_dw"][k])
    return vecs


def _consts():
    c = np.zeros((128, NCST), np.float64)
    c[:, C_ID:C_ID + 128] = np.eye(128)
    lg = np.log1p(-np.exp2(-5.0 - np.arange(4, dtype=np.float64)))
    idx = np.arange(128, dtype=np.float64)
    for h in range(4):
        rel = idx[None, :] - idx[:, None]
        c[:, C_MASK + h * 128:C_MASK + (h + 1) * 128] = np.where(rel >= 0, np.exp(lg[h] * np.maximum(rel, 0)), 0.0) / 16.0
        c[:, C_QDEC + h * 128:C_QDEC + (h + 1) * 128] = np.exp(lg[h] * (idx + 1.0))[None, :]
        c[:, C_KDEC + h] = np.exp(lg[h] * (127.0 - idx)) / 16.0
    for g in range(4):
        win = 2 << g
        t = np.arange(16, dtype=np.float64)
        c[:, C_WOC + g * 16:C_WOC + (g + 1) * 16] = (win / np.minimum(t + 1.0, win))[None, :]
    c[:, C_INVF] = (np.float32(10000.0) ** (-np.arange(128, dtype=np.float32) / np.float32(128))).astype(np.float64)
    return c.astype(np.float32)


_CACHE = {}


def _run(inp, S, layers=(0, 1, 2, 3), mlp=True, ncores=8, mix=True):
    key = (S, tuple(layers), mlp, mix)
    if key not in _CACHE:
        _CACHE[key] = Builder(S, layers, mlp, mix).build()
    nc = _CACHE[key]
    wblk = _pack_weights(inp)
    vecs = _pack_vecs(inp)
    cst = _consts()
    x = np.asarray(inp["x"], np.float32)
    pos = np.asarray(inp["positions"], np.int32)
    in_maps = []
    for b in range(ncores):
        in_maps.append({"xT": np.ascontiguousarray(x[b, :S].T), "wblk": wblk, "vecs": vecs, "cst": cst,
                        "pos": np.ascontiguousarray(pos[b, :S].reshape(1, S))})
    res = run_bass_kernel_spmd(nc, in_maps, core_ids=list(range(ncores)))
    out = np.stack([np.ascontiguousarray(res.results[b]["outT"].T) for b in range(ncores)], 0)
    return out.astype(np.float32)


def kernel(**inputs):
    inp = {k: np.asarray(v) for k, v in inputs.items()}
    return _run(inp, 4096)
```

```python
import numpy as np
from contextlib import ExitStack
import concourse.bass as bass
import concourse.mybir as mybir
from concourse.bass_utils import run_bass_kernel_spmd

F32 = mybir.dt.float32
BF16 = mybir.dt.bfloat16
I32 = mybir.dt.int32
ALU = mybir.AluOpType
AF = mybir.ActivationFunctionType

D = 1024
T = 512
EPS = 1e-6
UNIT = 256
NRING = 3
TWO_PI_HI = 6.28125
TWO_PI_LO = float(2.0 * np.pi - 6.28125)

BLK = {}
_names = (["conv_in0", "conv_in1", "cdw0", "cdw1", "cdw2", "cdw3", "conv_out"] + ["up0_%d" % i for i in range(4)] + ["dn0_%d" % i for i in range(4)]
          + ["up1_%d" % i for i in range(4)] + ["dn1_%d" % i for i in range(4)]
          + ["sc_in0", "sc_in1", "sc_in2", "scdw", "sc_out"] + ["up2_%d" % i for i in range(4)] + ["dn2_%d" % i for i in range(4)]
          + ["ret_in%d" % i for i in range(6)] + ["ret_out0", "ret_out1"]
          + ["up3_%d" % i for i in range(4)] + ["dn3_%d" % i for i in range(4)] + ["pool", "pm0", "pm1", "pm0_t0", "pm1_t0"])
for _i, _n in enumerate(_names):
    BLK[_n] = _i
NB = len(_names)

def V_NG(l, j, c): return (l * 4 + j) * 8 + c
V_CBIN = 128
def V_CDW(k, c): return 144 + k * 8 + c
V_CDWB = 392; V_CLNG = 400; V_CLNB = 408; V_CBOUT = 416; V_PSCALE = 424
def V_SCDW(k, c): return 432 + k * 8 + c
NV = 456
C_ID = 0; C_MASK = 128; C_QDEC = 640; C_KDEC = 1152; C_WOC = 1156; C_INVF = 1220
NCST = 1221


class View:
    __slots__ = ("ap", "spans")
    def __init__(self, ap, spans):
        self.ap = ap; self.spans = spans


class Buf:
    def __init__(self, name, handle, nbytes):
        self.name = name; self.h = handle; self.nbytes = nbytes
    def f32(self, off, n):
        return View(self.h[:, off:off + n], [(self.name, off * 4, (off + n) * 4)])
    def bf(self, off, n):
        lo = off - (off % 2); hi = off + n + ((off + n) % 2)
        ap = self.h[:, lo // 2:hi // 2].bitcast(BF16)
        if lo != off or hi != off + n:
            ap = ap[:, off - lo:off - lo + n]
        return View(ap, [(self.name, off * 2, (off + n) * 2)])
    def i32(self, off, n):
        return View(self.h[:, off:off + n].bitcast(I32), [(self.name, off * 4, (off + n) * 4)])
    def f32_3d(self, nchunk, width, c0, c1, lo, hi):
        ap = self.h[:, 0:nchunk * width].rearrange("p (c n) -> p c n", c=nchunk)[:, c0:c1, lo:hi]
        return View(ap, [(self.name, (c * width + lo) * 4, (c * width + hi) * 4) for c in range(c0, c1)])
    def bf_3d(self, nchunk, width, c0, c1, lo, hi):
        ap = self.h[:, 0:nchunk * width // 2].bitcast(BF16).rearrange("p (c n) -> p c n", c=nchunk)[:, c0:c1, lo:hi]
        return View(ap, [(self.name, (c * width + lo) * 2, (c * width + hi) * 2) for c in range(c0, c1)])


class Op:
    __slots__ = ("eng", "fn", "deps", "dma", "ticket", "sem", "semval", "signal")
    def __init__(self, eng, fn, dma):
        self.eng = eng; self.fn = fn; self.dma = dma; self.deps = {}
        self.ticket = None; self.sem = None; self.semval = None; self.signal = False


class Sched:
    def __init__(self):
        self.ops = []
        self.lastw = {}
        self.readers = {}

    @staticmethod
    def _units(views):
        for v in views:
            for (b, lo, hi) in v.spans:
                for u in range(lo // UNIT, (hi - 1) // UNIT + 1):
                    yield (b, u)

    def add(self, eng, fn, reads=(), writes=(), dma=False):
        idx = len(self.ops)
        op = Op(eng, fn, dma)
        deps = op.deps
        ru = list(self._units(reads)); wu = list(self._units(writes))
        for k in ru:
            w = self.lastw.get(k)
            if w is not None:
                deps[w] = True
        for k in wu:
            w = self.lastw.get(k)
            if w is not None and w not in deps:
                deps[w] = False
            for r in self.readers.get(k, ()):
                if r not in deps:
                    deps[r] = False
        for k in ru:
            self.readers.setdefault(k, []).append(idx)
        for k in wu:
            self.lastw[k] = idx
            self.readers[k] = []
        self.ops.append(op)
        return idx

    def finalize(self, dma_sems):
        ops = self.ops
        for op in ops:
            need = {}
            for d, raw in op.deps.items():
                dop = ops[d]
                if not dop.dma and not op.dma and dop.eng == op.eng:
                    if op.eng == "pe" or not raw:
                        continue
                need[d] = raw
            op.deps = need
            for d in need:
                ops[d].signal = True
        cnt = {}
        dcnt = {}
        for op in ops:
            if op.dma:
                q = op.eng
                n = dcnt.get(q, 0); dcnt[q] = n + 1
                sems = dma_sems[q]
                op.sem = sems[n % len(sems)]
                op.semval = 16 * (n // len(sems) + 1)
            elif op.signal:
                c = cnt.get(op.eng, 0) + 1
                cnt[op.eng] = c
                op.ticket = c

    def emit(self, eng, e, eng_sems, dma_sems):
        ops = self.ops
        waited = {}
        def wait(sem, val):
            key = id(sem)
            if waited.get(key, 0) >= val:
                return
            waited[key] = val
            e.wait_ge(sem, val)
        for op in ops:
            if op.eng != eng:
                continue
            best = {}
            for d in op.deps:
                dop = ops[d]
                if dop.dma:
                    wait(dop.sem, dop.semval)
                else:
                    if best.get(dop.eng, 0) < dop.ticket:
                        best[dop.eng] = dop.ticket
            for en, tk in best.items():
                wait(eng_sems[en], tk)
            if op.dma and op.semval > 16:
                wait(op.sem, op.semval - 16)
            if op.fn is None:
                continue
            ins = op.fn(e)
            if op.dma:
                ins.then_inc(op.sem, 16)
            elif op.signal:
                ins.then_inc(eng_sems[eng], 1)


class Builder:
    def __init__(self, S, layers=(0, 1, 2, 3), mlp=True, mix=True):
        self.S = S
        self.NT = S // T
        self.layers = layers
        self.do_mlp = mlp
        self.do_mix = mix
        self.s = Sched()

    def act(self, out, in_, func, bias=None, scale=None, accum=None):
        kw = {}
        reads = [in_]
        if bias is not None:
            if isinstance(bias, View):
                kw["bias"] = bias.ap; reads.append(bias)
            else:
                kw["bias"] = self.cbias(bias); reads.append(self.cbias_view(bias))
        if scale is not None:
            if isinstance(scale, View):
                kw["scale"] = scale.ap; reads.append(scale)
            else:
                kw["scale"] = float(scale)
        writes = [out]
        if accum is not None:
            kw["accum_out"] = accum.ap; writes.append(accum)
        self.s.add("act", lambda e: e.activation(out=out.ap, in_=in_.ap, func=func, **kw), reads, writes)

    def cbias_view(self, val):
        return self.cb[val]
    def cbias(self, val):
        return self.cb[val].ap

    def tt(self, out, a, b, op, eng="dve"):
        self.s.add(eng, lambda e: e.tensor_tensor(out=out.ap, in0=a.ap, in1=b.ap, op=op), [a, b], [out])

    def ts(self, out, a, s1, op0, s2=None, op1=None, eng="dve"):
        reads = [a]
        v1 = s1.ap if isinstance(s1, View) else float(s1)
        if isinstance(s1, View): reads.append(s1)
        v2 = None
        if s2 is not None:
            v2 = s2.ap if isinstance(s2, View) else float(s2)
            if isinstance(s2, View): reads.append(s2)
        if op1 is None:
            self.s.add(eng, lambda e: e.tensor_scalar(out=out.ap, in0=a.ap, scalar1=v1, scalar2=None, op0=op0), reads, [out])
        else:
            self.s.add(eng, lambda e: e.tensor_scalar(out=out.ap, in0=a.ap, scalar1=v1, scalar2=v2, op0=op0, op1=op1), reads, [out])

    def stt(self, out, a, sc, b, op0, op1):
        reads = [a, b]
        v = sc.ap if isinstance(sc, View) else float(sc)
        if isinstance(sc, View): reads.append(sc)
        self.s.add("dve", lambda e: e.scalar_tensor_tensor(out=out.ap, in0=a.ap, scalar=v, in1=b.ap, op0=op0, op1=op1), reads, [out])

    def copy(self, out, a, eng="dve"):
        if eng == "act":
            self.s.add("act", lambda e: e.copy(out=out.ap, in_=a.ap), [a], [out])
        else:
            self.s.add(eng, lambda e: e.tensor_copy(out=out.ap, in_=a.ap), [a], [out])

    def memset(self, out, val, eng="dve"):
        self.s.add(eng, lambda e: e.memset(out.ap, val), [], [out])

    def mm(self, out, lhsT, rhs, start, stop):
        self.s.add("pe", lambda e: e.matmul(out.ap, lhsT=lhsT.ap, rhs=rhs.ap, start=start, stop=stop), [lhsT, rhs], [out])

    def tr(self, out, in_):
        ident = self.ident
        self.s.add("pe", lambda e: e.transpose(out.ap, in_.ap, ident.ap), [in_, ident], [out])

    def dma(self, q, out, in_):
        self.s.add(q, lambda e: e.dma_start(out=out.ap, in_=in_.ap), [in_], [out], dma=True)

    def mmbank(self):
        b = self.psb[self._mmi % 6]; self._mmi += 1
        return b
    def stbank(self):
        b = self.psb[6 + self._sti % 2]; self._sti += 1
        return b
    def tmp(self):
        i = self._tmpi % 4; self._tmpi += 1
        return self.tmpf.f32(i * 544, 512), i

    def vec(self, col):
        return self.vecs.f32(col, 1)

    def wget(self, name):
        i = self._wpos
        assert self.worder[i] == BLK[name], (name, i)
        while self._wissued < min(len(self.worder), i + NRING):
            k = self._wissued
            blk = self.worder[k]
            slot = self.ring.bf((k % NRING) * 8192, 8192)
            wkey = View(self.wbf[blk], [("wbf%d" % blk, 0, 1)])
            if self._first_use[blk] == k:
                self.dma("pool", slot, View(self._cast_src[blk], [("wblk", 0, 1)]))
                if self.worder.count(blk) > 1:
                    self.dma("sp", wkey, slot)
            else:
                self.dma("sp", slot, wkey)
            self._wissued += 1
        self._wpos += 1
        return (i % NRING) * 8192

    def wv(self, base, off, n):
        return self.ring.bf(base + off, n)

    def stats(self, src_bf):
        ps = self.stbank().f32(0, 512)
        for c in range(8):
            self.mm(ps, self.onesw, src_bf.bf(c * 512, 512), c == 0, c == 7)
        return ps

    def rsqrt_from(self, out, ps):
        self.act(out, ps, AF.Ln, bias=EPS)
        self.act(out, out, AF.Exp, scale=-0.5)

    def prenorm(self, l, j, to_big=False):
        self.act(self.sqb.bf(0, 4096), self.h.f32(0, 4096), AF.Square)
        ps = self.stats(self.sqb)
        self.rsqrt_from(self.rstd, ps)
        for c in range(8):
            if to_big:
                out = self.big.f32(c * 544 + 32, 512)
            else:
                out = self.xn.bf(c * 512, 512)
            self.stt(out, self.h.f32(c * 512, 512), self.vec(V_NG(l, j, c)), self.rstd, ALU.mult, ALU.mult)

    def postnorm(self, l, j, final=False):
        ps = self.stats(self.sqb)
        self.rsqrt_from(self.rstd, ps)
        for c in range(8):
            uc = self.u.f32(c * 512, 512)
            self.stt(uc, uc, self.vec(V_NG(l, j, c)), self.rstd, ALU.mult, ALU.mult)
            hc = self.h.f32(c * 512, 512)
            self.tt(uc if final else hc, hc, uc, ALU.add)

    def evac_u(self, oc, ps, bias=None, scale=None):
        uo = self.u.f32(oc * 512, 512)
        if bias is not None:
            self.ts(uo, ps, bias, ALU.add)
            self.act(self.sqb.bf(oc * 512, 512), uo, AF.Square)
        elif scale is not None:
            self.act(uo, ps, AF.Identity, scale=scale)
            self.act(self.sqb.bf(oc * 512, 512), ps, AF.Square, scale=scale)
        else:
            self.act(uo, ps, AF.Copy)
            self.act(self.sqb.bf(oc * 512, 512), ps, AF.Square)

    def proj_block(self, wb, rhs_buf, evac_fn, defer=None, kouter=False):
        first = 0
        if kouter:
            banks = [self.mmbank().f32(0, 512) for _ in range(6)]
            for kc in range(8):
                for oc in range(6):
                    self.mm(banks[oc], self.wv(wb, kc * 1024 + oc * 128, 128), rhs_buf.bf(kc * 512, 512), kc == 0, kc == 7)
            if defer is not None and not defer["done"]:
                defer["fn"](); defer["done"] = True
            for oc in range(6):
                evac_fn(oc, banks[oc])
            first = 6
        pend = []
        for oc in range(first, 8):
            ps = self.mmbank().f32(0, 512)
            for kc in range(8):
                self.mm(ps, self.wv(wb, kc * 1024 + oc * 128, 128), rhs_buf.bf(kc * 512, 512), kc == 0, kc == 7)
            if defer is not None and not defer["done"]:
                pend.append((oc, ps))
                if oc == defer["after"]:
                    defer["fn"](); defer["done"] = True
                    for (o, p) in pend: evac_fn(o, p)
                    pend = []
            else:
                evac_fn(oc, ps)

    def mlp(self, l, final=False, hook=None):
        for c in range(8):
            self.act(self.xn.bf(c * 512, 512), self.h.f32(c * 512, 512), AF.Identity, scale=self.vec(V_NG(l, 2, c)))
        r2 = self.st1
        for b in range(4):
            wb = self.wget("up%d_%d" % (l, b))
            def evac_relu2(oc, ps, b=b):
                t, _ = self.tmp()
                self.act(t, ps, AF.Relu)
                self.tt(self.mid.bf((b * 8 + oc) * 512, 512), t, t, ALU.mult)
            self.proj_block(wb, self.xn, evac_relu2, kouter=(b == 0))
            if b == 0:
                self.act(self.sqb.bf(0, 4096), self.h.f32(0, 4096), AF.Square)
                ps = self.stats(self.sqb)
                self.act(r2, ps, AF.Ln, bias=EPS)
                self.act(r2, r2, AF.Exp, scale=-1.0)
            if b == 1 and hook is not None:
                hook()
        for db in range(4):
            wb = self.wget("dn%d_%d" % (l, db))
            for o2 in range(2):
                oc = db * 2 + o2
                ps = self.mmbank().f32(0, 512)
                for kc in range(32):
                    self.mm(ps, self.wv(wb, kc * 256 + o2 * 128, 128), self.mid.bf(kc * 512, 512), kc == 0, kc == 31)
                uo = self.u.f32(oc * 512, 512)
                self.tt(uo, ps, r2, ALU.mult)
                self.act(self.sqb.bf(oc * 512, 512), uo, AF.Square)
        self.postnorm(l, 3, final=final)

    def conformer(self, ti):
        GB = 8192
        for c in range(8):
            self.act(self.xn.bf(c * 512, 512), self.h.f32(c * 512, 512), AF.Identity, scale=self.vec(V_NG(0, 0, c)))
        self.act(self.sqb.bf(0, 4096), self.h.f32(0, 4096), AF.Square)
        def stats_chain():
            ps = self.stats(self.sqb)
            self.rsqrt_from(self.rstd, ps)
        self.copy(View(
            self.mid.h[:, GB // 2:(GB + 8 * 544) // 2].bitcast(BF16).rearrange("p (c n) -> p c n", c=8)[:, :, 2:32],
            [("mid", (GB + c * 544 + 2) * 2, (GB + c * 544 + 32) * 2) for c in range(8)]),
            self.halo0.bf_3d(8, 30, 0, 8, 0, 30), eng="pool")
        wb = self.wget("conv_in0")
        def evac_a(oc, ps):
            t, _ = self.tmp()
            self.tt(t, ps, self.rstd, ALU.mult)
            self.act(self.big.f32(oc * 544 + 32, 512), t, AF.Identity, bias=self.vec(V_CBIN + oc))
        self.proj_block(wb, self.xn, evac_a, {"after": 2, "fn": stats_chain, "done": False}, kouter=True)
        wb = self.wget("conv_in1")
        for oc in range(8):
            ps = self.mmbank().f32(0, 512)
            for kc in range(8):
                self.mm(ps, self.wv(wb, kc * 1024 + oc * 128, 128), self.xn.bf(kc * 512, 512), kc == 0, kc == 7)
            t, _ = self.tmp()
            self.tt(t, ps, self.rstd, ALU.mult)
            self.act(t, t, AF.Sigmoid, bias=self.vec(V_CBIN + 8 + oc))
            self.tt(self.mid.bf(GB + oc * 544 + 32, 512), self.big.f32(oc * 544 + 32, 512), t, ALU.mult)
        self.copy(self.halo0.bf_3d(8, 30, 0, 8, 0, 30), View(
            self.mid.h[:, GB // 2:(GB + 8 * 544) // 2].bitcast(BF16).rearrange("p (c n) -> p c n", c=8)[:, :, 514:544],
            [("mid", (GB + c * 544 + 514) * 2, (GB + c * 544 + 544) * 2) for c in range(8)]), eng="pool")
        for b in range(4):
            wb = self.wget("cdw%d" % b)
            for cc in range(2):
                c = 2 * b + cc
                ps = self.mmbank().f32(0, 512)
                for k in range(31):
                    self.mm(ps, self.wv(wb, (cc * 31 + k) * 128, 128), self.mid.bf(GB + c * 544 + 2 + k, 512), k == 0, k == 30)
                uc = self.u.f32(c * 512, 512)
                self.ts(uc, ps, self.vec(V_CDWB + c), ALU.add)
                self.act(self.xn.bf(c * 512, 512), uc, AF.Copy)
                self.act(self.sqb.bf(c * 512, 512), uc, AF.Square)
        psm = self.stats(self.xn)
        pss = self.stats(self.sqb)
        self.act(self.st1, psm, AF.Copy)
        t, _ = self.tmp()
        self.stt(t, self.st1, -1.0, self.st1, ALU.mult, ALU.mult)
        self.tt(t, t, pss, ALU.add)
        self.rsqrt_from(self.rstd, t)
        for c in range(8):
            t, _ = self.tmp()
            self.tt(t, self.u.f32(c * 512, 512), self.st1, ALU.subtract)
            self.stt(t, t, self.vec(V_CLNG + c), self.rstd, ALU.mult, ALU.mult)
            self.act(self.mid.bf(c * 512, 512), t, AF.Silu, bias=self.vec(V_CLNB + c))
        wb = self.wget("conv_out")
        self.proj_block(wb, self.mid, lambda oc, ps: self.evac_u(oc, ps, bias=self.vec(V_CBOUT + oc)), kouter=True)
        self.postnorm(0, 1)

    def poolmix(self, ti):
        for c in range(8):
            self.act(self.xn.bf(c * 512, 512), self.h.f32(c * 512, 512), AF.Identity, scale=self.vec(V_NG(1, 0, c)))
        self.act(self.sqb.bf(0, 4096), self.h.f32(0, 4096), AF.Square)
        rT = self.misc.f32(1600, 4)
        def stats_chain():
            pT = self.stbank()
            for j in range(4):
                for kc in range(8):
                    self.mm(pT.f32(j, 1), self.sqb.bf(kc * 512 + j * 128, 128), self.misc.bf(2944, 1), kc == 0, kc == 7)
            self.act(rT, pT.f32(0, 4), AF.Ln, bias=EPS)
            self.act(rT, rT, AF.Exp, scale=-0.5)
        pend = []
        done = False
        for j in range(4):
            for gp in range(2):
                psb = self.mmbank()
                for gl in range(2):
                    g = gp * 2 + gl
                    for kc in range(2):
                        self.mm(psb.f32(gl * 256, 256), self.xn.bf((2 * g + kc) * 512 + j * 128, 128),
                                self.pw.bf((g * 2 + kc) * 256, 256), kc == 0, kc == 1)
                pend.append((j, gp, psb))
                if not done and len(pend) == 2:
                    stats_chain(); done = True
                if done:
                    for (jj, gg, pb) in pend:
                        self.act(self.mid.bf(jj * 1024 + gg * 512, 512), pb.f32(0, 512), AF.Identity,
                                 scale=self.misc.f32(1600 + jj, 1))
                    pend = []
        for half in range(2):
            wb = self.wget(("pm%d_t0" if ti == 0 else "pm%d") % half)
            for gl in range(2):
                g = half * 2 + gl
                for eh in range(2):
                    c = 2 * g + eh
                    ps = self.mmbank().f32(0, 512)
                    for j in range(4):
                        self.mm(ps, self.mid.bf(j * 1024 + c * 128, 128), self.wv(wb, (gl * 5 + j) * 512, 512),
                                j == 0, (j == 3 and ti == 0))
                    if ti > 0:
                        self.mm(ps, self.zh.bf(c * 128, 128), self.wv(wb, (gl * 5 + 4) * 512, 512), False, True)
                    self.evac_u(c, ps, scale=self.vec(V_PSCALE + c))
        self.copy(self.zh.bf(0, 1024), self.mid.bf(3 * 1024, 1024), eng="pool")
        self.postnorm(1, 1)

    def shortconv(self, ti):
        GB = 8192
        def gbv(lo, hi):
            return View(self.mid.h[:, GB // 2:(GB + 8 * 544) // 2].bitcast(BF16).rearrange("p (c n) -> p c n", c=8)[:, :, lo:hi],
                        [("mid", (GB + c * 544 + lo) * 2, (GB + c * 544 + hi) * 2) for c in range(8)])
        for c in range(8):
            self.act(self.xn.bf(c * 512, 512), self.h.f32(c * 512, 512), AF.Identity, scale=self.vec(V_NG(2, 0, c)))
        self.act(self.sqb.bf(0, 4096), self.h.f32(0, 4096), AF.Square)
        r2 = self.st1
        def stats_chain():
            ps = self.stats(self.sqb)
            self.act(self.rstd, ps, AF.Ln, bias=EPS)
            self.act(r2, self.rstd, AF.Exp, scale=-1.0)
            self.act(self.rstd, self.rstd, AF.Exp, scale=-0.5)
        self.copy(gbv(30, 32), self.halo2.bf_3d(8, 2, 0, 8, 0, 2), eng="pool")
        defer = {"after": 2, "fn": stats_chain, "done": False}
        wb = self.wget("sc_in0")
        self.proj_block(wb, self.xn, lambda oc, ps: self.tt(self.u.f32(oc * 512, 512), ps, self.rstd, ALU.mult), defer, kouter=True)
        wb = self.wget("sc_in1")
        self.proj_block(wb, self.xn, lambda oc, ps: self.tt(self.big.f32(oc * 544 + 32, 512), ps, r2, ALU.mult))
        wb = self.wget("sc_in2")
        self.proj_block(wb, self.xn, lambda oc, ps: self.tt(self.mid.bf(GB + oc * 544 + 32, 512), self.big.f32(oc * 544 + 32, 512), ps, ALU.mult))
        self.copy(self.halo2.bf_3d(8, 2, 0, 8, 0, 2), gbv(542, 544), eng="pool")
        wb = self.wget("scdw")
        for c in range(8):
            ps = self.mmbank().f32(0, 512)
            for k in range(3):
                self.mm(ps, self.wv(wb, (c * 3 + k) * 128, 128), self.mid.bf(GB + c * 544 + 30 + k, 512), k == 0, k == 2)
            self.tt(self.mid.bf(c * 512, 512), self.u.f32(c * 512, 512), ps, ALU.mult)
        wb = self.wget("sc_out")
        self.proj_block(wb, self.mid, lambda oc, ps: self.evac_u(oc, ps), kouter=True)
        self.postnorm(2, 1)

    def sincos(self, ti):
        t0 = ti * T
        pi_, _ = self.tmp()
        posi = View(pi_.ap.bitcast(I32), pi_.spans)
        src = View(self.pos[0:1, t0:t0 + T].partition_broadcast(128), [("pos", 0, 1)])
        self.dma("sp", posi, src)
        ang, _ = self.tmp()
        self.copy(ang, posi)
        self.ts(ang, ang, self.cst.f32(C_INVF, 1), ALU.mult)
        for which in range(2):
            a2, _ = self.tmp()
            if which == 0:
                self.ts(a2, ang, 1.0, ALU.mult)
            else:
                self.ts(a2, ang, float(np.pi / 2), ALU.add)
            kk, _ = self.tmp()
            ki = View(kk.ap.bitcast(I32), kk.spans)
            self.ts(ki, a2, float(1.0 / (2 * np.pi)), ALU.mult)
            dst = self.cs.f32(which * 512, 512)
            self.copy(dst, ki)
            self.stt(a2, dst, -TWO_PI_HI, a2, ALU.mult, ALU.add)
            self.stt(a2, dst, -TWO_PI_LO, a2, ALU.mult, ALU.add)
            self.ts(a2, a2, -3.1415925, ALU.max, 3.1415925, ALU.min)
            self.act(dst, a2, AF.Sin)

    def rotary_block(self, wb, dst_off, pre=None):
        sin = self.cs.f32(0, 512); cos = self.cs.f32(512, 512)
        banks = {}
        if pre is not None:
            for oc in range(6):
                banks[oc] = self.mmbank().f32(0, 512)
            for kc in range(8):
                for oc in range(6):
                    self.mm(banks[oc], self.wv(wb, kc * 1024 + oc * 128, 128), self.xn.bf(kc * 512, 512), kc == 0, kc == 7)
            pre()
        for hh in range(4):
            pss = []
            for half in range(2):
                oc = 2 * hh + half
                if oc in banks:
                    pss.append(banks[oc]); continue
                ps = self.mmbank().f32(0, 512)
                for kc in range(8):
                    self.mm(ps, self.wv(wb, kc * 1024 + oc * 128, 128), self.xn.bf(kc * 512, 512), kc == 0, kc == 7)
                pss.append(ps)
            t1, _ = self.tmp(); t2, _ = self.tmp(); t3, _ = self.tmp(); t4, _ = self.tmp()
            self.tt(t1, pss[0], cos, ALU.mult)
            self.tt(t2, pss[1], sin, ALU.mult)
            self.tt(t3, pss[0], sin, ALU.mult)
            self.tt(t4, pss[1], cos, ALU.mult)
            self.tt(self.mid.bf(dst_off + (2 * hh) * 512, 512), t1, t2, ALU.subtract, eng="pool")
            self.tt(self.mid.bf(dst_off + (2 * hh + 1) * 512, 512), t3, t4, ALU.add, eng="pool")

    def retention(self, ti):
        QR, QD, KR, KT = 0, 4096, 8192, 12288
        if not (self.do_mlp and 2 in self.layers):
            self.sincos(ti)
        for c in range(8):
            self.act(self.xn.bf(c * 512, 512), self.h.f32(c * 512, 512), AF.Identity, scale=self.vec(V_NG(3, 0, c)))
        self.act(self.sqb.bf(0, 4096), self.h.f32(0, 4096), AF.Square)
        rT = self.misc.f32(1600, 4)
        def stats_chain():
            ps = self.stats(self.sqb)
            self.rsqrt_from(self.rstd, ps)
            for w in range(2):
                csw = self.cs.f32(w * 512, 512)
                self.tt(csw, csw, self.rstd, ALU.mult)
            pT = self.stbank()
            for j in range(4):
                for kc in range(8):
                    self.mm(pT.f32(j, 1), self.sqb.bf(kc * 512 + j * 128, 128), self.misc.bf(2944, 1), kc == 0, kc == 7)
            self.act(rT, pT.f32(0, 4), AF.Ln, bias=EPS)
            self.act(rT, rT, AF.Exp, scale=-0.5)
        wb = self.wget("ret_in0")
        self.rotary_block(wb, QR, pre=stats_chain)
        for c in range(8):
            hh = c // 2
            for j in range(4):
                self.tt(self.mid.bf(QD + c * 512 + j * 128, 128), self.mid.bf(QR + c * 512 + j * 128, 128),
                        self.cst.f32(C_QDEC + hh * 128, 128), ALU.mult, eng="pool")
        wb = self.wget("ret_in1")
        self.rotary_block(wb, KR)
        for j in range(4):
            for hh in range(4):
                psb = self.mmbank()
                for dc in range(2):
                    self.tr(psb.bf(dc * 128, 128), self.mid.bf(KR + (2 * hh + dc) * 512 + j * 128, 128))
                self.act(self.mid.bf(KT + (j * 4 + hh) * 256, 256), psb.bf(0, 256), AF.Identity,
                         scale=self.cst.f32(C_KDEC + hh, 1))
        for b in range(2):
            wb = self.wget("ret_in%d" % (2 + b))
            for j in range(4):
                for h2 in range(2):
                    hh = 2 * b + h2
                    ps = self.mmbank().f32(0, 512)
                    for kc in range(8):
                        self.mm(ps, self.xn.bf(kc * 512 + j * 128, 128), self.wv(wb, kc * 1024 + h2 * 512, 512), kc == 0, kc == 7)
                    self.act(self.u.bf((j * 4 + hh) * 512, 512), ps, AF.Identity, scale=self.misc.f32(1600 + j, 1))
        for j in range(4):
            ogoff = (j % 2) * 2048
            scs = []
            for hh in range(4):
                pS = self.mmbank().f32(0, 128)
                for dc in range(2):
                    self.mm(pS, self.mid.bf(KR + (2 * hh + dc) * 512 + j * 128, 128),
                            self.mid.bf(QR + (2 * hh + dc) * 512 + j * 128, 128), dc == 0, dc == 1)
                sc = self.scb.bf(hh * 128, 128)
                self.tt(sc, pS, self.cst.f32(C_MASK + hh * 128, 128), ALU.mult)
                scs.append(sc)
            pOs = []
            for hh in range(4):
                vt = self.u.bf((j * 4 + hh) * 512, 512)
                pO = self.mmbank().f32(0, 512)
                self.mm(pO, scs[hh], vt, True, False)
                for dc in range(2):
                    self.mm(pO, self.mid.bf(QD + (2 * hh + dc) * 512 + j * 128, 128),
                            self.sbf.bf((hh * 2 + dc) * 512, 512), False, dc == 1)
                ss = self.misc.f32(1152 + (self._ssi % 4) * 64, 1); self._ssi += 1
                junk, _ = self.tmp()
                self.act(junk, pO, AF.Square, accum=ss)
                self.act(ss, ss, AF.Ln, bias=EPS, scale=1.0 / 512)
                self.act(ss, ss, AF.Exp, scale=-0.5)
                self.ts(self.sqb.bf(ogoff + hh * 512, 512), pO, ss, ALU.mult)
            for hh in range(4):
                vt = self.u.bf((j * 4 + hh) * 512, 512)
                for dc in range(2):
                    pD = self.mmbank().f32(0, 512)
                    self.mm(pD, self.mid.bf(KT + (j * 4 + hh) * 256 + dc * 128, 128), vt, True, True)
                    st = self.st.f32((hh * 2 + dc) * 512, 512)
                    self.stt(st, st, float(self.g128[hh]), pD, ALU.mult, ALU.add)
                    self.copy(self.sbf.bf((hh * 2 + dc) * 512, 512), st, eng="pool")
            for half in range(2):
                psb = self.mmbank()
                for f in range(8):
                    fc = half * 8 + f
                    self.tr(psb.bf(f * 128, 128), self.sqb.bf(ogoff + fc * 128, 128))
                self.copy(self.big.bf_3d(16, 512, half * 8, half * 8 + 8, j * 128, j * 128 + 128),
                          View(psb.bf(0, 1024).ap.rearrange("p (c n) -> p c n", c=8), psb.bf(0, 1024).spans), eng="dve")
        for b in range(2):
            wb = self.wget("ret_in%d" % (4 + b))
            for oc in range(8):
                fc = b * 8 + oc
                ps = self.mmbank().f32(0, 512)
                for kc in range(8):
                    self.mm(ps, self.wv(wb, kc * 1024 + oc * 128, 128), self.xn.bf(kc * 512, 512), kc == 0, kc == 7)
                t, _ = self.tmp()
                self.tt(t, ps, self.rstd, ALU.mult)
                self.act(t, t, AF.Silu)
                o = self.big.bf(fc * 512, 512)
                self.tt(o, o, t, ALU.mult)
        for b in range(2):
            wb = self.wget("ret_out%d" % b)
            for o4 in range(4):
                oc = b * 4 + o4
                ps = self.mmbank().f32(0, 512)
                for kc in range(16):
                    self.mm(ps, self.wv(wb, kc * 512 + o4 * 128, 128), self.big.bf(kc * 512, 512), kc == 0, kc == 15)
                self.evac_u(oc, ps)
        self.postnorm(3, 1)

    def build(self):
        S = self.S
        nc = bass.Bass("TRN2", target_bir_lowering=False)
        self.nc = nc
        xT = nc.dram_tensor("xT", [D, S], F32, kind="ExternalInput").ap()
        outT = nc.dram_tensor("outT", [D, S], F32, kind="ExternalOutput").ap()
        wblk = nc.dram_tensor("wblk", [NB, 128, 8192], F32, kind="ExternalInput").ap()
        self.wbf = nc.dram_tensor("wbf", [NB, 128, 8192], BF16, kind="Internal").ap()
        vecs_d = nc.dram_tensor("vecs", [128, NV], F32, kind="ExternalInput").ap()
        cst_d = nc.dram_tensor("cst", [128, NCST], F32, kind="ExternalInput").ap()
        self.pos = nc.dram_tensor("pos", [1, S], I32, kind="ExternalInput").ap()
        lg = np.log1p(-np.exp2(-5.0 - np.arange(4, dtype=np.float64)))
        self.g128 = np.exp(lg * 128.0)

        def order_for(ti):
            o = []
            for l in range(4):
                if l in self.layers:
                    if l == 0 and self.do_mix: o += ["conv_in0", "conv_in1", "cdw0", "cdw1", "cdw2", "cdw3", "conv_out"]
                    if l == 1 and self.do_mix: o += (["pm0_t0", "pm1_t0"] if ti == 0 else ["pm0", "pm1"])
                    if l == 2 and self.do_mix: o += ["sc_in0", "sc_in1", "sc_in2", "scdw", "sc_out"]
                    if l == 3 and self.do_mix: o += ["ret_in%d" % i for i in range(6)] + ["ret_out0", "ret_out1"]
                    if self.do_mlp:
                        o += ["up%d_%d" % (l, i) for i in range(4)] + ["dn%d_%d" % (l, i) for i in range(4)]
            return o
        order_tile = order_for(0)
        self.worder = []
        for ti in range(self.NT):
            self.worder += [BLK[n] for n in order_for(ti)]
        self._first_use = {}
        for k, b in enumerate(self.worder):
            self._first_use.setdefault(b, k)
        self._wpos = 0; self._wissued = 0
        self._mmi = 0; self._sti = 0; self._tmpi = 0; self._sci = 0; self._ssi = 0

        with ExitStack() as es:
            def sb(name, nbytes):
                hnd = es.enter_context(nc.sbuf_tensor(name, [128, nbytes // 4], F32))
                return Buf(name, hnd, nbytes)
            self.h = sb("h", 16384)
            self.xn = sb("xn", 8192)
            self.mid = sb("mid", 32768)
            self.u = sb("u", 16384)
            self.sqb = sb("sqb", 8192)
            self.big = sb("big", 8 * 544 * 4)
            self.tmpf = sb("tmpf", 4 * 544 * 4)
            self.ring = sb("ring", NRING * 16384)
            self.st = sb("st", 16384)
            self.sbf = sb("sbf", 8192)
            self.vecs = sb("vecs_sb", NV * 4)
            self.cst = sb("cst_sb", NCST * 4)
            self.pw = sb("pw", 4096)
            self.cs = sb("cs", 4096)
            self.halo0 = sb("halo0", 8 * 30 * 2)
            self.zh = sb("zh", 2048)
            self.halo2 = sb("halo2", 64)
            self.scb = sb("scb", 1024)
            misc = sb("misc", 6656)
            self.misc = misc
            self.rstd = misc.f32(0, 512)
            self.st1 = misc.f32(512, 512)
            self.onesw = misc.bf(2944, 128)
            self.ident = misc.bf(3072, 128)
            self.cb = {EPS: misc.f32(1408, 1)}
            self.psb = []
            for i in range(8):
                hnd = es.enter_context(nc.psum_tensor("ps%d" % i, [128, 512], F32))
                self.psb.append(Buf("ps%d" % i, hnd, 2048))
            eng_sems = {}
            for en in ("pe", "act", "dve", "pool", "sp"):
                eng_sems[en] = es.enter_context(nc.semaphore("sem_" + en))
            dma_sems = {"sp": [es.enter_context(nc.semaphore("dsp%d" % i)) for i in range(12)],
                        "pool": [es.enter_context(nc.semaphore("dpl%d" % i)) for i in range(16)]}

            self.dma("sp", self.vecs.f32(0, NV), View(vecs_d[:, :], [("vecs_d", 0, 1)]))
            self.dma("sp", self.cst.f32(0, NCST), View(cst_d[:, :], [("cst_d", 0, 1)]))
            self.memset(self.onesw, 1.0 / 1024)
            self.memset(self.cb[EPS], EPS)
            self.copy(self.ident, self.cst.f32(C_ID, 128))
            self.memset(self.st.f32(0, 4096), 0.0)
            self.memset(self.sbf.bf(0, 4096), 0.0)
            self.memset(self.halo0.f32(0, 120), 0.0)
            self.memset(self.halo2.f32(0, 16), 0.0)
            self.memset(self.big.f32(0, 8 * 544), 0.0)
            self._cast_src = wblk
            self._ntile_blocks = len(order_tile)
            if 1 in self.layers and self.do_mix:
                self.dma("pool", self.pw.bf(0, 2048), View(wblk[BLK["pool"]][:, 0:2048], [("wblk", 0, 1)]))

            last_l = max(self.layers) if self.layers else -1
            for ti in range(self.NT):
                t0 = ti * T
                for c in range(8):
                    self.dma("sp", self.h.f32(c * 512, 512), View(xT[c * 128:(c + 1) * 128, t0:t0 + T], [("xT", 0, 1)]))
                for l in range(4):
                    if l not in self.layers:
                        continue
                    if not self.do_mix: pass
                    elif l == 0: self.conformer(ti)
                    elif l == 1: self.poolmix(ti)
                    elif l == 2: self.shortconv(ti)
                    else: self.retention(ti)
                    if self.do_mlp:
                        hk = None
                        if l == 2 and 3 in self.layers and self.do_mix:
                            hk = (lambda ti=ti: self.sincos(ti))
                        self.mlp(l, final=(l == last_l), hook=hk)
                src = self.u if (self.do_mlp and last_l in self.layers) else self.h
                for c in range(8):
                    self.dma("pool", View(outT[c * 128:(c + 1) * 128, t0:t0 + T], [("outT%d_%d" % (ti, c), 0, 1)]), src.f32(c * 512, 512))
            self.s.add("pool", None, reads=[View(None, [("outT%d_%d" % (ti, c), 0, 1) for ti in range(self.NT) for c in range(8)])], writes=[])

            self.s.finalize(dma_sems)
            sched = self.s
            with nc.Block() as block:
                @block.sync
                def _(e): sched.emit("sp", e, eng_sems, dma_sems)
                @block.tensor
                def _(e): sched.emit("pe", e, eng_sems, dma_sems)
                @block.scalar
                def _(e): sched.emit("act", e, eng_sems, dma_sems)
                @block.vector
                def _(e): sched.emit("dve", e, eng_sems, dma_sems)
                @block.gpsimd
                def _(e): sched.emit("pool", e, eng_sems, dma_sems)
        return nc


def _kblock(W, col0, ncols):
    K = W.shape[0]
    sub = W[:, col0:col0 + ncols].reshape(K // 128, 128, ncols)
    out = np.ascontiguousarray(sub.transpose(1, 0, 2)).reshape(128, (K // 128) * ncols)
    if out.shape[1] < 8192:
        out = np.concatenate([out, np.zeros((128, 8192 - out.shape[1]), np.float32)], axis=1)
    return out


def _pack_weights(inp):
    blocks = [None] * NB
    for i in range(2): blocks[BLK["conv_in%d" % i]] = _kblock(inp["conv_w_in"], i * 1024, 1024)
    blocks[BLK["conv_out"]] = _kblock(inp["conv_w_out"], 0, 1024)
    dw = inp["conv_dw"]
    pidx = np.arange(128)
    for b in range(4):
        blk = np.zeros((128, 8192), np.float32)
        for cc in range(2):
            c = 2 * b + cc
            for k in range(31):
                d = cc * 31 + k
                blk[pidx, d * 128 + pidx] = dw[k, c * 128:(c + 1) * 128]
        blocks[BLK["cdw%d" % b]] = blk
    for i in range(3): blocks[BLK["sc_in%d" % i]] = _kblock(inp["sc_w_in"], i * 1024, 1024)
    blocks[BLK["sc_out"]] = _kblock(inp["sc_w_out"], 0, 1024)
    blk = np.zeros((128, 8192), np.float32)
    for c in range(8):
        for k in range(3):
            blk[np.arange(128), (c * 3 + k) * 128 + np.arange(128)] = inp["sc_dw"][k, c * 128:(c + 1) * 128]
    blocks[BLK["scdw"]] = blk
    for i in range(6): blocks[BLK["ret_in%d" % i]] = _kblock(inp["ret_w_in"], i * 1024, 1024)
    for i in range(2): blocks[BLK["ret_out%d" % i]] = _kblock(inp["ret_w_out"], i * 512, 512)
    for l in range(4):
        for i in range(4):
            blocks[BLK["up%d_%d" % (l, i)]] = _kblock(inp["mlp_up"][l], i * 1024, 1024)
            blocks[BLK["dn%d_%d" % (l, i)]] = _kblock(inp["mlp_down"][l], i * 256, 256)
    for first in (False, True):
        pm = [np.zeros((128, 8192), np.float32) for _ in range(2)]
        for g in range(4):
            w = 2 << g
            M = np.zeros((512, 640), np.float64)
            for t in range(512):
                lo = t - w + 1
                if first: lo = max(lo, 0)
                cnt = (t - lo + 1) if first else w
                M[t, lo + 128:t + 129] = 1.0 / cnt
                M[t, t + 128] -= 1.0
            MT = M.T
            half, gl = divmod(g, 2)
            for j in range(4):
                pm[half][:, (gl * 5 + j) * 512:(gl * 5 + j + 1) * 512] = MT[128 + j * 128:256 + j * 128, :]
            pm[half][:, (gl * 5 + 4) * 512:(gl * 5 + 5) * 512] = MT[0:128, :]
        for half in range(2):
            blocks[BLK[("pm%d_t0" if first else "pm%d") % half]] = pm[half]
    pw = inp["pool_w"]
    pb = np.ascontiguousarray(pw.reshape(4, 2, 128, 256).transpose(2, 0, 1, 3)).reshape(128, 2048)
    blocks[BLK["pool"]] = np.concatenate([pb, np.zeros((128, 8192 - 2048), np.float32)], axis=1)
    return np.ascontiguousarray(np.stack(blocks, 0).astype(np.float32))


def _col(v):
    return np.ascontiguousarray(np.asarray(v, np.float32).reshape(-1, 128).T)


def _pack_vecs(inp):
    vecs = np.zeros((128, NV), np.float32)
    ng = inp["norm_g"]
    for l in range(4):
        for j in range(4):
            vecs[:, V_NG(l, j, 0):V_NG(l, j, 0) + 8] = _col(ng[l, j])
    vecs[:, V_CBIN:V_CBIN + 16] = _col(inp["conv_b_in"])
    for k in range(31):
        vecs[:, V_CDW(k, 0):V_CDW(k, 0) + 8] = _col(inp["conv_dw"][k])
    vecs[:, V_CDWB:V_CDWB + 8] = _col(inp["conv_dw_b"])
    vecs[:, V_CLNG:V_CLNG + 8] = _col(inp["conv_ln_g"])
    vecs[:, V_CLNB:V_CLNB + 8] = _col(inp["conv_ln_b"])
    vecs[:, V_CBOUT:V_CBOUT + 8] = _col(inp["conv_b_out"])
    vecs[:, V_PSCALE:V_PSCALE + 8] = _col(inp["pool_scale"])
    for k in range(3):
        vecs[:, V_SCDW(k, 0):V_SCDW(k, 0) + 8] = _col(inp["sc_dw"][k])
    return vecs


def _consts():
    c = np.zeros((128, NCST), np.float64)
    c[:, C_ID:C_ID + 128] = np.eye(128)
    lg = np.log1p(-np.exp2(-5.0 - np.arange(4, dtype=np.float64)))
    idx = np.arange(128, dtype=np.float64)
    for h in range(4):
        rel = idx[None, :] - idx[:, None]
        c[:, C_MASK + h * 128:C_MASK + (h + 1) * 128] = np.where(rel >= 0, np.exp(lg[h] * np.maximum(rel, 0)), 0.0) / 16.0
        c[:, C_QDEC + h * 128:C_QDEC + (h + 1) * 128] = np.exp(lg[h] * (idx + 1.0))[None, :]
        c[:, C_KDEC + h] = np.exp(lg[h] * (127.0 - idx)) / 16.0
    for g in range(4):
        win = 2 << g
        t = np.arange(16, dtype=np.float64)
        c[:, C_WOC + g * 16:C_WOC + (g + 1) * 16] = (win / np.minimum(t + 1.0, win))[None, :]
    c[:, C_INVF] = (np.float32(10000.0) ** (-np.arange(128, dtype=np.float32) / np.float32(128))).astype(np.float64)
    return c.astype(np.float32)


_CACHE = {}


def _run(inp, S, layers=(0, 1, 2, 3), mlp=True, ncores=8, mix=True):
    key = (S, tuple(layers), mlp, mix)
    if key not in _CACHE:
        _CACHE[key] = Builder(S, layers, mlp, mix).build()
    nc = _CACHE[key]
    wblk = _pack_weights(inp)
    vecs = _pack_vecs(inp)
    cst = _consts()
    x = np.asarray(inp["x"], np.float32)
    pos = np.asarray(inp["positions"], np.int32)
    in_maps = []
    for b in range(ncores):
        in_maps.append({"xT": np.ascontiguousarray(x[b, :S].T), "wblk": wblk, "vecs": vecs, "cst": cst,
                        "pos": np.ascontiguousarray(pos[b, :S].reshape(1, S))})
    res = run_bass_kernel_spmd(nc, in_maps, core_ids=list(range(ncores)))
    out = np.stack([np.ascontiguousarray(res.results[b]["outT"].T) for b in range(ncores)], 0)
    return out.astype(np.float32)


def kernel(**inputs):
    inp = {k: np.asarray(v) for k, v in inputs.items()}
    return _run(inp, 4096)
```

```python
import numpy as np
from contextlib import ExitStack
import concourse.bass as bass
import concourse.mybir as mybir
from concourse.bass_utils import run_bass_kernel_spmd

F32 = mybir.dt.float32
BF16 = mybir.dt.bfloat16
I32 = mybir.dt.int32
ALU = mybir.AluOpType
AF = mybir.ActivationFunctionType

D = 1024
T = 512
EPS = 1e-6
UNIT = 256
NRING = 3
TWO_PI_HI = 6.28125
TWO_PI_LO = float(2.0 * np.pi - 6.28125)

BLK = {}
_names = (["conv_in0", "conv_in1", "cdw0", "cdw1", "cdw2", "cdw3", "conv_out"] + ["up0_%d" % i for i in range(4)] + ["dn0_%d" % i for i in range(4)]
          + ["up1_%d" % i for i in range(4)] + ["dn1_%d" % i for i in range(4)]
          + ["sc_in0", "sc_in1", "sc_in2", "scdw", "sc_out"] + ["up2_%d" % i for i in range(4)] + ["dn2_%d" % i for i in range(4)]
          + ["ret_in%d" % i for i in range(6)] + ["ret_out0", "ret_out1"]
          + ["up3_%d" % i for i in range(4)] + ["dn3_%d" % i for i in range(4)] + ["pool", "pm0", "pm1", "pm0_t0", "pm1_t0"])
for _i, _n in enumerate(_names):
    BLK[_n] = _i
NB = len(_names)

def V_NG(l, j, c): return (l * 4 + j) * 8 + c
V_CBIN = 128
def V_CDW(k, c): return 144 + k * 8 + c
V_CDWB = 392; V_CLNG = 400; V_CLNB = 408; V_CBOUT = 416; V_PSCALE = 424
def V_SCDW(k, c): return 432 + k * 8 + c
NV = 456
C_ID = 0; C_MASK = 128; C_QDEC = 640; C_KDEC = 1152; C_WOC = 1156; C_INVF = 1220
NCST = 1221


class View:
    __slots__ = ("ap", "spans")
    def __init__(self, ap, spans):
        self.ap = ap; self.spans = spans


class Buf:
    def __init__(self, name, handle, nbytes):
        self.name = name; self.h = handle; self.nbytes = nbytes
    def f32(self, off, n):
        return View(self.h[:, off:off + n], [(self.name, off * 4, (off + n) * 4)])
    def bf(self, off, n):
        lo = off - (off % 2); hi = off + n + ((off + n) % 2)
        ap = self.h[:, lo // 2:hi // 2].bitcast(BF16)
        if lo != off or hi != off + n:
            ap = ap[:, off - lo:off - lo + n]
        return View(ap, [(self.name, off * 2, (off + n) * 2)])
    def i32(self, off, n):
        return View(self.h[:, off:off + n].bitcast(I32), [(self.name, off * 4, (off + n) * 4)])
    def f32_3d(self, nchunk, width, c0, c1, lo, hi):
        ap = self.h[:, 0:nchunk * width].rearrange("p (c n) -> p c n", c=nchunk)[:, c0:c1, lo:hi]
        return View(ap, [(self.name, (c * width + lo) * 4, (c * width + hi) * 4) for c in range(c0, c1)])
    def bf_3d(self, nchunk, width, c0, c1, lo, hi):
        ap = self.h[:, 0:nchunk * width // 2].bitcast(BF16).rearrange("p (c n) -> p c n", c=nchunk)[:, c0:c1, lo:hi]
        return View(ap, [(self.name, (c * width + lo) * 2, (c * width + hi) * 2) for c in range(c0, c1)])


class Op:
    __slots__ = ("eng", "fn", "deps", "dma", "ticket", "sem", "semval", "signal")
    def __init__(self, eng, fn, dma):
        self.eng = eng; self.fn = fn; self.dma = dma; self.deps = {}
        self.ticket = None; self.sem = None; self.semval = None; self.signal = False


class Sched:
    def __init__(self):
        self.ops = []
        self.lastw = {}
        self.readers = {}

    @staticmethod
    def _units(views):
        for v in views:
            for (b, lo, hi) in v.spans:
                for u in range(lo // UNIT, (hi - 1) // UNIT + 1):
                    yield (b, u)

    def add(self, eng, fn, reads=(), writes=(), dma=False):
        idx = len(self.ops)
        op = Op(eng, fn, dma)
        deps = op.deps
        ru = list(self._units(reads)); wu = list(self._units(writes))
        for k in ru:
            w = self.lastw.get(k)
            if w is not None:
                deps[w] = True
        for k in wu:
            w = self.lastw.get(k)
            if w is not None and w not in deps:
                deps[w] = False
            for r in self.readers.get(k, ()):
                if r not in deps:
                    deps[r] = False
        for k in ru:
            self.readers.setdefault(k, []).append(idx)
        for k in wu:
            self.lastw[k] = idx
            self.readers[k] = []
        self.ops.append(op)
        return idx

    def finalize(self, dma_sems):
        ops = self.ops
        for op in ops:
            need = {}
            for d, raw in op.deps.items():
                dop = ops[d]
                if not dop.dma and not op.dma and dop.eng == op.eng:
                    if op.eng == "pe" or not raw:
                        continue
                need[d] = raw
            op.deps = need
            for d in need:
                ops[d].signal = True
        cnt = {}
        dcnt = {}
        for op in ops:
            if op.dma:
                q = op.eng
                n = dcnt.get(q, 0); dcnt[q] = n + 1
                sems = dma_sems[q]
                op.sem = sems[n % len(sems)]
                op.semval = 16 * (n // len(sems) + 1)
            elif op.signal:
                c = cnt.get(op.eng, 0) + 1
                cnt[op.eng] = c
                op.ticket = c

    def emit(self, eng, e, eng_sems, dma_sems):
        ops = self.ops
        waited = {}
        def wait(sem, val):
            key = id(sem)
            if waited.get(key, 0) >= val:
                return
            waited[key] = val
            e.wait_ge(sem, val)
        for op in ops:
            if op.eng != eng:
                continue
            best = {}
            for d in op.deps:
                dop = ops[d]
                if dop.dma:
                    wait(dop.sem, dop.semval)
                else:
                    if best.get(dop.eng, 0) < dop.ticket:
                        best[dop.eng] = dop.ticket
            for en, tk in best.items():
                wait(eng_sems[en], tk)
            if op.dma and op.semval > 16:
                wait(op.sem, op.semval - 16)
            if op.fn is None:
                continue
            ins = op.fn(e)
            if op.dma:
                ins.then_inc(op.sem, 16)
            elif op.signal:
                ins.then_inc(eng_sems[eng], 1)


class Builder:
    def __init__(self, S, layers=(0, 1, 2, 3), mlp=True, mix=True):
        self.S = S
        self.NT = S // T
        self.layers = layers
        self.do_mlp = mlp
        self.do_mix = mix
        self.s = Sched()

    def act(self, out, in_, func, bias=None, scale=None, accum=None):
        kw = {}
        reads = [in_]
        if bias is not None:
            if isinstance(bias, View):
                kw["bias"] = bias.ap; reads.append(bias)
            else:
                kw["bias"] = self.cbias(bias); reads.append(self.cbias_view(bias))
        if scale is not None:
            if isinstance(scale, View):
                kw["scale"] = scale.ap; reads.append(scale)
            else:
                kw["scale"] = float(scale)
        writes = [out]
        if accum is not None:
            kw["accum_out"] = accum.ap; writes.append(accum)
        self.s.add("act", lambda e: e.activation(out=out.ap, in_=in_.ap, func=func, **kw), reads, writes)

    def cbias_view(self, val):
        return self.cb[val]
    def cbias(self, val):
        return self.cb[val].ap

    def tt(self, out, a, b, op, eng="dve"):
        self.s.add(eng, lambda e: e.tensor_tensor(out=out.ap, in0=a.ap, in1=b.ap, op=op), [a, b], [out])

    def ts(self, out, a, s1, op0, s2=None, op1=None, eng="dve"):
        reads = [a]
        v1 = s1.ap if isinstance(s1, View) else float(s1)
        if isinstance(s1, View): reads.append(s1)
        v2 = None
        if s2 is not None:
            v2 = s2.ap if isinstance(s2, View) else float(s2)
            if isinstance(s2, View): reads.append(s2)
        if op1 is None:
            self.s.add(eng, lambda e: e.tensor_scalar(out=out.ap, in0=a.ap, scalar1=v1, scalar2=None, op0=op0), reads, [out])
        else:
            self.s.add(eng, lambda e: e.tensor_scalar(out=out.ap, in0=a.ap, scalar1=v1, scalar2=v2, op0=op0, op1=op1), reads, [out])

    def stt(self, out, a, sc, b, op0, op1):
        reads = [a, b]
        v = sc.ap if isinstance(sc, View) else float(sc)
        if isinstance(sc, View): reads.append(sc)
        self.s.add("dve", lambda e: e.scalar_tensor_tensor(out=out.ap, in0=a.ap, scalar=v, in1=b.ap, op0=op0, op1=op1), reads, [out])

    def copy(self, out, a, eng="dve"):
        if eng == "act":
            self.s.add("act", lambda e: e.copy(out=out.ap, in_=a.ap), [a], [out])
        else:
            self.s.add(eng, lambda e: e.tensor_copy(out=out.ap, in_=a.ap), [a], [out])

    def memset(self, out, val, eng="dve"):
        self.s.add(eng, lambda e: e.memset(out.ap, val), [], [out])

    def mm(self, out, lhsT, rhs, start, stop):
        self.s.add("pe", lambda e: e.matmul(out.ap, lhsT=lhsT.ap, rhs=rhs.ap, start=start, stop=stop), [lhsT, rhs], [out])

    def tr(self, out, in_):
        ident = self.ident
        self.s.add("pe", lambda e: e.transpose(out.ap, in_.ap, ident.ap), [in_, ident], [out])

    def dma(self, q, out, in_):
        self.s.add(q, lambda e: e.dma_start(out=out.ap, in_=in_.ap), [in_], [out], dma=True)

    def mmbank(self):
        b = self.psb[self._mmi % 6]; self._mmi += 1
        return b
    def stbank(self):
        b = self.psb[6 + self._sti % 2]; self._sti += 1
        return b
    def tmp(self):
        i = self._tmpi % 4; self._tmpi += 1
        return self.tmpf.f32(i * 544, 512), i

    def vec(self, col):
        return self.vecs.f32(col, 1)

    def wget(self, name):
        i = self._wpos
        assert self.worder[i] == BLK[name], (name, i)
        while self._wissued < min(len(self.worder), i + NRING):
            k = self._wissued
            blk = self.worder[k]
            slot = self.ring.bf((k % NRING) * 8192, 8192)
            wkey = View(self.wbf[blk], [("wbf%d" % blk, 0, 1)])
            if self._first_use[blk] == k:
                self.dma("pool", slot, View(self._cast_src[blk], [("wblk", 0, 1)]))
                if self.worder.count(blk) > 1:
                    self.dma("sp", wkey, slot)
            else:
                self.dma("sp", slot, wkey)
            self._wissued += 1
        self._wpos += 1
        return (i % NRING) * 8192

    def wv(self, base, off, n):
        return self.ring.bf(base + off, n)

    def stats(self, src_bf):
        ps = self.stbank().f32(0, 512)
        for c in range(8):
            self.mm(ps, self.onesw, src_bf.bf(c * 512, 512), c == 0, c == 7)
        return ps

    def rsqrt_from(self, out, ps):
        self.act(out, ps, AF.Ln, bias=EPS)
        self.act(out, out, AF.Exp, scale=-0.5)

    def prenorm(self, l, j, to_big=False):
        self.act(self.sqb.bf(0, 4096), self.h.f32(0, 4096), AF.Square)
        ps = self.stats(self.sqb)
        self.rsqrt_from(self.rstd, ps)
        for c in range(8):
            if to_big:
                out = self.big.f32(c * 544 + 32, 512)
            else:
                out = self.xn.bf(c * 512, 512)
            self.stt(out, self.h.f32(c * 512, 512), self.vec(V_NG(l, j, c)), self.rstd, ALU.mult, ALU.mult)

    def postnorm(self, l, j, final=False):
        ps = self.stats(self.sqb)
        self.rsqrt_from(self.rstd, ps)
        for c in range(8):
            uc = self.u.f32(c * 512, 512)
            self.stt(uc, uc, self.vec(V_NG(l, j, c)), self.rstd, ALU.mult, ALU.mult)
            hc = self.h.f32(c * 512, 512)
            self.tt(uc if final else hc, hc, uc, ALU.add)

    def evac_u(self, oc, ps, bias=None, scale=None):
        uo = self.u.f32(oc * 512, 512)
        if bias is not None:
            self.ts(uo, ps, bias, ALU.add)
            self.act(self.sqb.bf(oc * 512, 512), uo, AF.Square)
        elif scale is not None:
            self.act(uo, ps, AF.Identity, scale=scale)
            self.act(self.sqb.bf(oc * 512, 512), ps, AF.Square, scale=scale)
        else:
            self.act(uo, ps, AF.Copy)
            self.act(self.sqb.bf(oc * 512, 512), ps, AF.Square)

    def proj_block(self, wb, rhs_buf, evac_fn, defer=None, kouter=False):
        first = 0
        if kouter:
            banks = [self.mmbank().f32(0, 512) for _ in range(6)]
            for kc in range(8):
                for oc in range(6):
                    self.mm(banks[oc], self.wv(wb, kc * 1024 + oc * 128, 128), rhs_buf.bf(kc * 512, 512), kc == 0, kc == 7)
            if defer is not None and not defer["done"]:
                defer["fn"](); defer["done"] = True
            for oc in range(6):
                evac_fn(oc, banks[oc])
            first = 6
        pend = []
        for oc in range(first, 8):
            ps = self.mmbank().f32(0, 512)
            for kc in range(8):
                self.mm(ps, self.wv(wb, kc * 1024 + oc * 128, 128), rhs_buf.bf(kc * 512, 512), kc == 0, kc == 7)
            if defer is not None and not defer["done"]:
                pend.append((oc, ps))
                if oc == defer["after"]:
                    defer["fn"](); defer["done"] = True
                    for (o, p) in pend: evac_fn(o, p)
                    pend = []
            else:
                evac_fn(oc, ps)

    def mlp(self, l, final=False, hook=None):
        for c in range(8):
            self.act(self.xn.bf(c * 512, 512), self.h.f32(c * 512, 512), AF.Identity, scale=self.vec(V_NG(l, 2, c)))
        r2 = self.st1
        for b in range(4):
            wb = self.wget("up%d_%d" % (l, b))
            def evac_relu2(oc, ps, b=b):
                t, _ = self.tmp()
                self.act(t, ps, AF.Relu)
                self.tt(self.mid.bf((b * 8 + oc) * 512, 512), t, t, ALU.mult)
            self.proj_block(wb, self.xn, evac_relu2, kouter=(b == 0))
            if b == 0:
                self.act(self.sqb.bf(0, 4096), self.h.f32(0, 4096), AF.Square)
                ps = self.stats(self.sqb)
                self.act(r2, ps, AF.Ln, bias=EPS)
                self.act(r2, r2, AF.Exp, scale=-1.0)
            if b == 1 and hook is not None:
                hook()
        for db in range(4):
            wb = self.wget("dn%d_%d" % (l, db))
            for o2 in range(2):
                oc = db * 2 + o2
                ps = self.mmbank().f32(0, 512)
                for kc in range(32):
                    self.mm(ps, self.wv(wb, kc * 256 + o2 * 128, 128), self.mid.bf(kc * 512, 512), kc == 0, kc == 31)
                uo = self.u.f32(oc * 512, 512)
                self.tt(uo, ps, r2, ALU.mult)
                self.act(self.sqb.bf(oc * 512, 512), uo, AF.Square)
        self.postnorm(l, 3, final=final)

    def conformer(self, ti):
        GB = 8192
        for c in range(8):
            self.act(self.xn.bf(c * 512, 512), self.h.f32(c * 512, 512), AF.Identity, scale=self.vec(V_NG(0, 0, c)))
            self.act(self.sqb.bf(c * 512, 512), self.h.f32(c * 512, 512), AF.Square)
        def stats_chain():
            ps = self.stats(self.sqb)
            self.rsqrt_from(self.rstd, ps)
        self.copy(View(
            self.mid.h[:, GB // 2:(GB + 8 * 544) // 2].bitcast(BF16).rearrange("p (c n) -> p c n", c=8)[:, :, 2:32],
            [("mid", (GB + c * 544 + 2) * 2, (GB + c * 544 + 32) * 2) for c in range(8)]),
            self.halo0.bf_3d(8, 30, 0, 8, 0, 30), eng="pool")
        wb = self.wget("conv_in0")
        def evac_a(oc, ps):
            t, _ = self.tmp()
            self.tt(t, ps, self.rstd, ALU.mult)
            self.act(self.big.f32(oc * 544 + 32, 512), t, AF.Identity, bias=self.vec(V_CBIN + oc))
        self.proj_block(wb, self.xn, evac_a, {"after": 2, "fn": stats_chain, "done": False}, kouter=True)
        wb = self.wget("conv_in1")
        for oc in range(8):
            ps = self.mmbank().f32(0, 512)
            for kc in range(8):
                self.mm(ps, self.wv(wb, kc * 1024 + oc * 128, 128), self.xn.bf(kc * 512, 512), kc == 0, kc == 7)
            t, _ = self.tmp()
            self.tt(t, ps, self.rstd, ALU.mult)
            self.act(t, t, AF.Sigmoid, bias=self.vec(V_CBIN + 8 + oc))
            self.tt(self.mid.bf(GB + oc * 544 + 32, 512), self.big.f32(oc * 544 + 32, 512), t, ALU.mult)
        self.copy(self.halo0.bf_3d(8, 30, 0, 8, 0, 30), View(
            self.mid.h[:, GB // 2:(GB + 8 * 544) // 2].bitcast(BF16).rearrange("p (c n) -> p c n", c=8)[:, :, 514:544],
            [("mid", (GB + c * 544 + 514) * 2, (GB + c * 544 + 544) * 2) for c in range(8)]), eng="pool")
        for b in range(4):
            wb = self.wget("cdw%d" % b)
            for cc in range(2):
                c = 2 * b + cc
                ps = self.mmbank().f32(0, 512)
                for k in range(31):
                    self.mm(ps, self.wv(wb, (cc * 31 + k) * 128, 128), self.mid.bf(GB + c * 544 + 2 + k, 512), k == 0, k == 30)
                uc = self.u.f32(c * 512, 512)
                self.ts(uc, ps, self.vec(V_CDWB + c), ALU.add)
                self.act(self.xn.bf(c * 512, 512), uc, AF.Copy)
                self.act(self.sqb.bf(c * 512, 512), uc, AF.Square)
        psm = self.stats(self.xn)
        pss = self.stats(self.sqb)
        self.act(self.st1, psm, AF.Copy)
        t, _ = self.tmp()
        self.stt(t, self.st1, -1.0, self.st1, ALU.mult, ALU.mult)
        self.tt(t, t, pss, ALU.add)
        self.rsqrt_from(self.rstd, t)
        for c in range(8):
            t, _ = self.tmp()
            self.tt(t, self.u.f32(c * 512, 512), self.st1, ALU.subtract)
            self.stt(t, t, self.vec(V_CLNG + c), self.rstd, ALU.mult, ALU.mult)
            self.act(self.mid.bf(c * 512, 512), t, AF.Silu, bias=self.vec(V_CLNB + c))
        wb = self.wget("conv_out")
        self.proj_block(wb, self.mid, lambda oc, ps: self.evac_u(oc, ps, bias=self.vec(V_CBOUT + oc)), kouter=True)
        self.postnorm(0, 1)

    def poolmix(self, ti):
        for c in range(8):
            self.act(self.xn.bf(c * 512, 512), self.h.f32(c * 512, 512), AF.Identity, scale=self.vec(V_NG(1, 0, c)))
            self.act(self.sqb.bf(c * 512, 512), self.h.f32(c * 512, 512), AF.Square)
        rT = self.misc.f32(1600, 4)
        def stats_chain():
            pT = self.stbank()
            for j in range(4):
                for kc in range(8):
                    self.mm(pT.f32(j, 1), self.sqb.bf(kc * 512 + j * 128, 128), self.misc.bf(2944, 1), kc == 0, kc == 7)
            self.act(rT, pT.f32(0, 4), AF.Ln, bias=EPS)
            self.act(rT, rT, AF.Exp, scale=-0.5)
        pend = []
        done = False
        for j in range(4):
            for gp in range(2):
                psb = self.mmbank()
                for gl in range(2):
                    g = gp * 2 + gl
                    for kc in range(2):
                        self.mm(psb.f32(gl * 256, 256), self.xn.bf((2 * g + kc) * 512 + j * 128, 128),
                                self.pw.bf((g * 2 + kc) * 256, 256), kc == 0, kc == 1)
                pend.append((j, gp, psb))
                if not done and len(pend) == 2:
                    stats_chain(); done = True
                if done:
                    for (jj, gg, pb) in pend:
                        self.act(self.mid.bf(jj * 1024 + gg * 512, 512), pb.f32(0, 512), AF.Identity,
                                 scale=self.misc.f32(1600 + jj, 1))
                    pend = []
        for half in range(2):
            wb = self.wget(("pm%d_t0" if ti == 0 else "pm%d") % half)
            for gl in range(2):
                g = half * 2 + gl
                for eh in range(2):
                    c = 2 * g + eh
                    ps = self.mmbank().f32(0, 512)
                    for j in range(4):
                        self.mm(ps, self.mid.bf(j * 1024 + c * 128, 128), self.wv(wb, (gl * 5 + j) * 512, 512),
                                j == 0, (j == 3 and ti == 0))
                    if ti > 0:
                        self.mm(ps, self.zh.bf(c * 128, 128), self.wv(wb, (gl * 5 + 4) * 512, 512), False, True)
                    self.evac_u(c, ps, scale=self.vec(V_PSCALE + c))
        self.copy(self.zh.bf(0, 1024), self.mid.bf(3 * 1024, 1024), eng="pool")
        self.postnorm(1, 1)

    def shortconv(self, ti):
        GB = 8192
        def gbv(lo, hi):
            return View(self.mid.h[:, GB // 2:(GB + 8 * 544) // 2].bitcast(BF16).rearrange("p (c n) -> p c n", c=8)[:, :, lo:hi],
                        [("mid", (GB + c * 544 + lo) * 2, (GB + c * 544 + hi) * 2) for c in range(8)])
        for c in range(8):
            self.act(self.xn.bf(c * 512, 512), self.h.f32(c * 512, 512), AF.Identity, scale=self.vec(V_NG(2, 0, c)))
            self.act(self.sqb.bf(c * 512, 512), self.h.f32(c * 512, 512), AF.Square)
        r2 = self.st1
        def stats_chain():
            ps = self.stats(self.sqb)
            self.act(self.rstd, ps, AF.Ln, bias=EPS)
            self.act(r2, self.rstd, AF.Exp, scale=-1.0)
            self.act(self.rstd, self.rstd, AF.Exp, scale=-0.5)
        self.copy(gbv(30, 32), self.halo2.bf_3d(8, 2, 0, 8, 0, 2), eng="pool")
        defer = {"after": 2, "fn": stats_chain, "done": False}
        wb = self.wget("sc_in0")
        self.proj_block(wb, self.xn, lambda oc, ps: self.tt(self.u.f32(oc * 512, 512), ps, self.rstd, ALU.mult), defer, kouter=True)
        wb = self.wget("sc_in1")
        self.proj_block(wb, self.xn, lambda oc, ps: self.tt(self.big.f32(oc * 544 + 32, 512), ps, r2, ALU.mult))
        wb = self.wget("sc_in2")
        self.proj_block(wb, self.xn, lambda oc, ps: self.tt(self.mid.bf(GB + oc * 544 + 32, 512), self.big.f32(oc * 544 + 32, 512), ps, ALU.mult))
        self.copy(self.halo2.bf_3d(8, 2, 0, 8, 0, 2), gbv(542, 544), eng="pool")
        wb = self.wget("scdw")
        for c in range(8):
            ps = self.mmbank().f32(0, 512)
            for k in range(3):
                self.mm(ps, self.wv(wb, (c * 3 + k) * 128, 128), self.mid.bf(GB + c * 544 + 30 + k, 512), k == 0, k == 2)
            self.tt(self.mid.bf(c * 512, 512), self.u.f32(c * 512, 512), ps, ALU.mult)
        wb = self.wget("sc_out")
        self.proj_block(wb, self.mid, lambda oc, ps: self.evac_u(oc, ps), kouter=True)
        self.postnorm(2, 1)

    def sincos(self, ti):
        t0 = ti * T
        pi_, _ = self.tmp()
        posi = View(pi_.ap.bitcast(I32), pi_.spans)
        src = View(self.pos[0:1, t0:t0 + T].partition_broadcast(128), [("pos", 0, 1)])
        self.dma("sp", posi, src)
        ang, _ = self.tmp()
        self.copy(ang, posi)
        self.ts(ang, ang, self.cst.f32(C_INVF, 1), ALU.mult)
        for which in range(2):
            a2, _ = self.tmp()
            if which == 0:
                self.ts(a2, ang, 1.0, ALU.mult)
            else:
                self.ts(a2, ang, float(np.pi / 2), ALU.add)
            kk, _ = self.tmp()
            ki = View(kk.ap.bitcast(I32), kk.spans)
            self.ts(ki, a2, float(1.0 / (2 * np.pi)), ALU.mult)
            dst = self.cs.f32(which * 512, 512)
            self.copy(dst, ki)
            self.stt(a2, dst, -TWO_PI_HI, a2, ALU.mult, ALU.add)
            self.stt(a2, dst, -TWO_PI_LO, a2, ALU.mult, ALU.add)
            self.ts(a2, a2, -3.1415925, ALU.max, 3.1415925, ALU.min)
            self.act(dst, a2, AF.Sin)

    def rotary_block(self, wb, dst_off, pre=None):
        sin = self.cs.f32(0, 512); cos = self.cs.f32(512, 512)
        banks = {}
        if pre is not None:
            for oc in range(6):
                banks[oc] = self.mmbank().f32(0, 512)
            for kc in range(8):
                for oc in range(6):
                    self.mm(banks[oc], self.wv(wb, kc * 1024 + oc * 128, 128), self.xn.bf(kc * 512, 512), kc == 0, kc == 7)
            pre()
        for hh in range(4):
            pss = []
            for half in range(2):
                oc = 2 * hh + half
                if oc in banks:
                    pss.append(banks[oc]); continue
                ps = self.mmbank().f32(0, 512)
                for kc in range(8):
                    self.mm(ps, self.wv(wb, kc * 1024 + oc * 128, 128), self.xn.bf(kc * 512, 512), kc == 0, kc == 7)
                pss.append(ps)
            t1, _ = self.tmp(); t2, _ = self.tmp(); t3, _ = self.tmp(); t4, _ = self.tmp()
            self.tt(t1, pss[0], cos, ALU.mult)
            self.tt(t2, pss[1], sin, ALU.mult)
            self.tt(t3, pss[0], sin, ALU.mult)
            self.tt(t4, pss[1], cos, ALU.mult)
            self.tt(self.mid.bf(dst_off + (2 * hh) * 512, 512), t1, t2, ALU.subtract, eng="pool")
            self.tt(self.mid.bf(dst_off + (2 * hh + 1) * 512, 512), t3, t4, ALU.add, eng="pool")

    def retention(self, ti):
        QR, QD, KR, KT = 0, 4096, 8192, 12288
        if not (self.do_mlp and 2 in self.layers):
            self.sincos(ti)
        for c in range(8):
            self.act(self.xn.bf(c * 512, 512), self.h.f32(c * 512, 512), AF.Identity, scale=self.vec(V_NG(3, 0, c)))
            self.act(self.sqb.bf(c * 512, 512), self.h.f32(c * 512, 512), AF.Square)
        rT = self.misc.f32(1600, 4)
        def stats_chain():
            ps = self.stats(self.sqb)
            self.rsqrt_from(self.rstd, ps)
            for w in range(2):
                csw = self.cs.f32(w * 512, 512)
                self.tt(csw, csw, self.rstd, ALU.mult)
            pT = self.stbank()
            for j in range(4):
                for kc in range(8):
                    self.mm(pT.f32(j, 1), self.sqb.bf(kc * 512 + j * 128, 128), self.misc.bf(2944, 1), kc == 0, kc == 7)
            self.act(rT, pT.f32(0, 4), AF.Ln, bias=EPS)
            self.act(rT, rT, AF.Exp, scale=-0.5)
        wb = self.wget("ret_in0")
        self.rotary_block(wb, QR, pre=stats_chain)
        for c in range(8):
            hh = c // 2
            for j in range(4):
                self.tt(self.mid.bf(QD + c * 512 + j * 128, 128), self.mid.bf(QR + c * 512 + j * 128, 128),
                        self.cst.f32(C_QDEC + hh * 128, 128), ALU.mult)
        wb = self.wget("ret_in1")
        self.rotary_block(wb, KR)
        for j in range(4):
            for hh in range(4):
                psb = self.mmbank()
                for dc in range(2):
                    self.tr(psb.bf(dc * 128, 128), self.mid.bf(KR + (2 * hh + dc) * 512 + j * 128, 128))
                self.act(self.mid.bf(KT + (j * 4 + hh) * 256, 256), psb.bf(0, 256), AF.Identity,
                         scale=self.cst.f32(C_KDEC + hh, 1))
        for b in range(2):
            wb = self.wget("ret_in%d" % (2 + b))
            for j in range(4):
                for h2 in range(2):
                    hh = 2 * b + h2
                    ps = self.mmbank().f32(0, 512)
                    for kc in range(8):
                        self.mm(ps, self.xn.bf(kc * 512 + j * 128, 128), self.wv(wb, kc * 1024 + h2 * 512, 512), kc == 0, kc == 7)
                    self.act(self.u.bf((j * 4 + hh) * 512, 512), ps, AF.Identity, scale=self.misc.f32(1600 + j, 1))
        for j in range(4):
            ogoff = (j % 2) * 2048
            scs = []
            for hh in range(4):
                pS = self.mmbank().f32(0, 128)
                for dc in range(2):
                    self.mm(pS, self.mid.bf(KR + (2 * hh + dc) * 512 + j * 128, 128),
                            self.mid.bf(QR + (2 * hh + dc) * 512 + j * 128, 128), dc == 0, dc == 1)
                sc = self.scb.bf(hh * 128, 128)
                self.tt(sc, pS, self.cst.f32(C_MASK + hh * 128, 128), ALU.mult)
                scs.append(sc)
            pOs = []
            for hh in range(4):
                vt = self.u.bf((j * 4 + hh) * 512, 512)
                pO = self.mmbank().f32(0, 512)
                self.mm(pO, scs[hh], vt, True, False)
                for dc in range(2):
                    self.mm(pO, self.mid.bf(QD + (2 * hh + dc) * 512 + j * 128, 128),
                            self.sbf.bf((hh * 2 + dc) * 512, 512), False, dc == 1)
                ss = self.misc.f32(1152 + (self._ssi % 4) * 64, 1); self._ssi += 1
                junk, _ = self.tmp()
                self.act(junk, pO, AF.Square, accum=ss)
                self.act(ss, ss, AF.Ln, bias=EPS, scale=1.0 / 512)
                self.act(ss, ss, AF.Exp, scale=-0.5)
                self.ts(self.sqb.bf(ogoff + hh * 512, 512), pO, ss, ALU.mult)
            for hh in range(4):
                vt = self.u.bf((j * 4 + hh) * 512, 512)
                for dc in range(2):
                    pD = self.mmbank().f32(0, 512)
                    self.mm(pD, self.mid.bf(KT + (j * 4 + hh) * 256 + dc * 128, 128), vt, True, True)
                    st = self.st.f32((hh * 2 + dc) * 512, 512)
                    self.stt(st, st, float(self.g128[hh]), pD, ALU.mult, ALU.add)
                    self.act(self.sbf.bf((hh * 2 + dc) * 512, 512), st, AF.Copy)
            for half in range(2):
                psb = self.mmbank()
                for f in range(8):
                    fc = half * 8 + f
                    self.tr(psb.bf(f * 128, 128), self.sqb.bf(ogoff + fc * 128, 128))
                self.copy(self.big.bf_3d(16, 512, half * 8, half * 8 + 8, j * 128, j * 128 + 128),
                          View(psb.bf(0, 1024).ap.rearrange("p (c n) -> p c n", c=8), psb.bf(0, 1024).spans), eng="dve")
        for b in range(2):
            wb = self.wget("ret_in%d" % (4 + b))
            for oc in range(8):
                fc = b * 8 + oc
                ps = self.mmbank().f32(0, 512)
                for kc in range(8):
                    self.mm(ps, self.wv(wb, kc * 1024 + oc * 128, 128), self.xn.bf(kc * 512, 512), kc == 0, kc == 7)
                t, _ = self.tmp()
                self.tt(t, ps, self.rstd, ALU.mult)
                self.act(t, t, AF.Silu)
                o = self.big.bf(fc * 512, 512)
                self.tt(o, o, t, ALU.mult)
        for b in range(2):
            wb = self.wget("ret_out%d" % b)
            for o4 in range(4):
                oc = b * 4 + o4
                ps = self.mmbank().f32(0, 512)
                for kc in range(16):
                    self.mm(ps, self.wv(wb, kc * 512 + o4 * 128, 128), self.big.bf(kc * 512, 512), kc == 0, kc == 15)
                self.evac_u(oc, ps)
        self.postnorm(3, 1)

    def build(self):
        S = self.S
        nc = bass.Bass("TRN2", target_bir_lowering=False)
        self.nc = nc
        xT = nc.dram_tensor("xT", [D, S], F32, kind="ExternalInput").ap()
        outT = nc.dram_tensor("outT", [D, S], F32, kind="ExternalOutput").ap()
        wblk = nc.dram_tensor("wblk", [NB, 128, 8192], F32, kind="ExternalInput").ap()
        self.wbf = nc.dram_tensor("wbf", [NB, 128, 8192], BF16, kind="Internal").ap()
        vecs_d = nc.dram_tensor("vecs", [128, NV], F32, kind="ExternalInput").ap()
        cst_d = nc.dram_tensor("cst", [128, NCST], F32, kind="ExternalInput").ap()
        self.pos = nc.dram_tensor("pos", [1, S], I32, kind="ExternalInput").ap()
        lg = np.log1p(-np.exp2(-5.0 - np.arange(4, dtype=np.float64)))
        self.g128 = np.exp(lg * 128.0)

        def order_for(ti):
            o = []
            for l in range(4):
                if l in self.layers:
                    if l == 0 and self.do_mix: o += ["conv_in0", "conv_in1", "cdw0", "cdw1", "cdw2", "cdw3", "conv_out"]
                    if l == 1 and self.do_mix: o += (["pm0_t0", "pm1_t0"] if ti == 0 else ["pm0", "pm1"])
                    if l == 2 and self.do_mix: o += ["sc_in0", "sc_in1", "sc_in2", "scdw", "sc_out"]
                    if l == 3 and self.do_mix: o += ["ret_in%d" % i for i in range(6)] + ["ret_out0", "ret_out1"]
                    if self.do_mlp:
                        o += ["up%d_%d" % (l, i) for i in range(4)] + ["dn%d_%d" % (l, i) for i in range(4)]
            return o
        order_tile = order_for(0)
        self.worder = []
        for ti in range(self.NT):
            self.worder += [BLK[n] for n in order_for(ti)]
        self._first_use = {}
        for k, b in enumerate(self.worder):
            self._first_use.setdefault(b, k)
        self._wpos = 0; self._wissued = 0
        self._mmi = 0; self._sti = 0; self._tmpi = 0; self._sci = 0; self._ssi = 0

        with ExitStack() as es:
            def sb(name, nbytes):
                hnd = es.enter_context(nc.sbuf_tensor(name, [128, nbytes // 4], F32))
                return Buf(name, hnd, nbytes)
            self.h = sb("h", 16384)
            self.xn = sb("xn", 8192)
            self.mid = sb("mid", 32768)
            self.u = sb("u", 16384)
            self.sqb = sb("sqb", 8192)
            self.big = sb("big", 8 * 544 * 4)
            self.tmpf = sb("tmpf", 4 * 544 * 4)
            self.ring = sb("ring", NRING * 16384)
            self.st = sb("st", 16384)
            self.sbf = sb("sbf", 8192)
            self.vecs = sb("vecs_sb", NV * 4)
            self.cst = sb("cst_sb", NCST * 4)
            self.pw = sb("pw", 4096)
            self.cs = sb("cs", 4096)
            self.halo0 = sb("halo0", 8 * 30 * 2)
            self.zh = sb("zh", 2048)
            self.halo2 = sb("halo2", 64)
            self.scb = sb("scb", 1024)
            misc = sb("misc", 6656)
            self.misc = misc
            self.rstd = misc.f32(0, 512)
            self.st1 = misc.f32(512, 512)
            self.onesw = misc.bf(2944, 128)
            self.ident = misc.bf(3072, 128)
            self.cb = {EPS: misc.f32(1408, 1)}
            self.psb = []
            for i in range(8):
                hnd = es.enter_context(nc.psum_tensor("ps%d" % i, [128, 512], F32))
                self.psb.append(Buf("ps%d" % i, hnd, 2048))
            eng_sems = {}
            for en in ("pe", "act", "dve", "pool", "sp"):
                eng_sems[en] = es.enter_context(nc.semaphore("sem_" + en))
            dma_sems = {"sp": [es.enter_context(nc.semaphore("dsp%d" % i)) for i in range(12)],
                        "pool": [es.enter_context(nc.semaphore("dpl%d" % i)) for i in range(16)]}

            self.dma("sp", self.vecs.f32(0, NV), View(vecs_d[:, :], [("vecs_d", 0, 1)]))
            self.dma("sp", self.cst.f32(0, NCST), View(cst_d[:, :], [("cst_d", 0, 1)]))
            self.memset(self.onesw, 1.0 / 1024)
            self.memset(self.cb[EPS], EPS)
            self.copy(self.ident, self.cst.f32(C_ID, 128))
            self.memset(self.st.f32(0, 4096), 0.0)
            self.memset(self.sbf.bf(0, 4096), 0.0)
            self.memset(self.halo0.f32(0, 120), 0.0)
            self.memset(self.halo2.f32(0, 16), 0.0)
            self.memset(self.big.f32(0, 8 * 544), 0.0)
            self._cast_src = wblk
            self._ntile_blocks = len(order_tile)
            if 1 in self.layers and self.do_mix:
                self.dma("pool", self.pw.bf(0, 2048), View(wblk[BLK["pool"]][:, 0:2048], [("wblk", 0, 1)]))

            last_l = max(self.layers) if self.layers else -1
            for ti in range(self.NT):
                t0 = ti * T
                for c in range(8):
                    self.dma("sp", self.h.f32(c * 512, 512), View(xT[c * 128:(c + 1) * 128, t0:t0 + T], [("xT", 0, 1)]))
                for l in range(4):
                    if l not in self.layers:
                        continue
                    if not self.do_mix: pass
                    elif l == 0: self.conformer(ti)
                    elif l == 1: self.poolmix(ti)
                    elif l == 2: self.shortconv(ti)
                    else: self.retention(ti)
                    if self.do_mlp:
                        hk = None
                        if l == 2 and 3 in self.layers and self.do_mix:
                            hk = (lambda ti=ti: self.sincos(ti))
                        self.mlp(l, final=(l == last_l), hook=hk)
                src = self.u if (self.do_mlp and last_l in self.layers) else self.h
                for c in range(8):
                    self.dma("pool", View(outT[c * 128:(c + 1) * 128, t0:t0 + T], [("outT%d_%d" % (ti, c), 0, 1)]), src.f32(c * 512, 512))
            self.s.add("pool", None, reads=[View(None, [("outT%d_%d" % (ti, c), 0, 1) for ti in range(self.NT) for c in range(8)])], writes=[])

            self.s.finalize(dma_sems)
            sched = self.s
            with nc.Block() as block:
                @block.sync
                def _(e): sched.emit("sp", e, eng_sems, dma_sems)
                @block.tensor
                def _(e): sched.emit("pe", e, eng_sems, dma_sems)
                @block.scalar
                def _(e): sched.emit("act", e, eng_sems, dma_sems)
                @block.vector
                def _(e): sched.emit("dve", e, eng_sems, dma_sems)
                @block.gpsimd
                def _(e): sched.emit("pool", e, eng_sems, dma_sems)
        return nc


def _kblock(W, col0, ncols):
    K = W.shape[0]
    sub = W[:, col0:col0 + ncols].reshape(K // 128, 128, ncols)
    out = np.ascontiguousarray(sub.transpose(1, 0, 2)).reshape(128, (K // 128) * ncols)
    if out.shape[1] < 8192:
        out = np.concatenate([out, np.zeros((128, 8192 - out.shape[1]), np.float32)], axis=1)
    return out


def _pack_weights(inp):
    blocks = [None] * NB
    for i in range(2): blocks[BLK["conv_in%d" % i]] = _kblock(inp["conv_w_in"], i * 1024, 1024)
    blocks[BLK["conv_out"]] = _kblock(inp["conv_w_out"], 0, 1024)
    dw = inp["conv_dw"]
    pidx = np.arange(128)
    for b in range(4):
        blk = np.zeros((128, 8192), np.float32)
        for cc in range(2):
            c = 2 * b + cc
            for k in range(31):
                d = cc * 31 + k
                blk[pidx, d * 128 + pidx] = dw[k, c * 128:(c + 1) * 128]
        blocks[BLK["cdw%d" % b]] = blk
    for i in range(3): blocks[BLK["sc_in%d" % i]] = _kblock(inp["sc_w_in"], i * 1024, 1024)
    blocks[BLK["sc_out"]] = _kblock(inp["sc_w_out"], 0, 1024)
    blk = np.zeros((128, 8192), np.float32)
    for c in range(8):
        for k in range(3):
            blk[np.arange(128), (c * 3 + k) * 128 + np.arange(128)] = inp["sc_dw"][k, c * 128:(c + 1) * 128]
    blocks[BLK["scdw"]] = blk
    for i in range(6): blocks[BLK["ret_in%d" % i]] = _kblock(inp["ret_w_in"], i * 1024, 1024)
    for i in range(2): blocks[BLK["ret_out%d" % i]] = _kblock(inp["ret_w_out"], i * 512, 512)
    for l in range(4):
        for i in range(4):
            blocks[BLK["up%d_%d" % (l, i)]] = _kblock(inp["mlp_up"][l], i * 1024, 1024)
            blocks[BLK["dn%d_%d" % (l, i)]] = _kblock(inp["mlp_down"][l], i * 256, 256)
    for first in (False, True):
        pm = [np.zeros((128, 8192), np.float32) for _ in range(2)]
        for g in range(4):
            w = 2 << g
            M = np.zeros((512, 640), np.float64)
            for t in range(512):
                lo = t - w + 1
                if first: lo = max(lo, 0)
                cnt = (t - lo + 1) if first else w
                M[t, lo + 128:t + 129] = 1.0 / cnt
                M[t, t + 128] -= 1.0
            MT = M.T
            half, gl = divmod(g, 2)
            for j in range(4):
                pm[half][:, (gl * 5 + j) * 512:(gl * 5 + j + 1) * 512] = MT[128 + j * 128:256 + j * 128, :]
            pm[half][:, (gl * 5 + 4) * 512:(gl * 5 + 5) * 512] = MT[0:128, :]
        for half in range(2):
            blocks[BLK[("pm%d_t0" if first else "pm%d") % half]] = pm[half]
    pw = inp["pool_w"]
    pb = np.ascontiguousarray(pw.reshape(4, 2, 128, 256).transpose(2, 0, 1, 3)).reshape(128, 2048)
    blocks[BLK["pool"]] = np.concatenate([pb, np.zeros((128, 8192 - 2048), np.float32)], axis=1)
    return np.ascontiguousarray(np.stack(blocks, 0).astype(np.float32))


def _col(v):
    return np.ascontiguousarray(np.asarray(v, np.float32).reshape(-1, 128).T)


def _pack_vecs(inp):
    vecs = np.zeros((128, NV), np.float32)
    ng = inp["norm_g"]
    for l in range(4):
        for j in range(4):
            vecs[:, V_NG(l, j, 0):V_NG(l, j, 0) + 8] = _col(ng[l, j])
    vecs[:, V_CBIN:V_CBIN + 16] = _col(inp["conv_b_in"])
    for k in range(31):
        vecs[:, V_CDW(k, 0):V_CDW(k, 0) + 8] = _col(inp["conv_dw"][k])
    vecs[:, V_CDWB:V_CDWB + 8] = _col(inp["conv_dw_b"])
    vecs[:, V_CLNG:V_CLNG + 8] = _col(inp["conv_ln_g"])
    vecs[:, V_CLNB:V_CLNB + 8] = _col(inp["conv_ln_b"])
    vecs[:, V_CBOUT:V_CBOUT + 8] = _col(inp["conv_b_out"])
    vecs[:, V_PSCALE:V_PSCALE + 8] = _col(inp["pool_scale"])
    for k in range(3):
        vecs[:, V_SCDW(k, 0):V_SCDW(k, 0) + 8] = _col(inp["sc_dw"][k])
    return vecs


def _consts():
    c = np.zeros((128, NCST), np.float64)
    c[:, C_ID:C_ID + 128] = np.eye(128)
    lg = np.log1p(-np.exp2(-5.0 - np.arange(4, dtype=np.float64)))
    idx = np.arange(128, dtype=np.float64)
    for h in range(4):
        rel = idx[None, :] - idx[:, None]
        c[:, C_MASK + h * 128:C_MASK + (h + 1) * 128] = np.where(rel >= 0, np.exp(lg[h] * np.maximum(rel, 0)), 0.0) / 16.0
        c[:, C_QDEC + h * 128:C_QDEC + (h + 1) * 128] = np.exp(lg[h] * (idx + 1.0))[None, :]
        c[:, C_KDEC + h] = np.exp(lg[h] * (127.0 - idx)) / 16.0
    for g in range(4):
        win = 2 << g
        t = np.arange(16, dtype=np.float64)
        c[:, C_WOC + g * 16:C_WOC + (g + 1) * 16] = (win / np.minimum(t + 1.0, win))[None, :]
    c[:, C_INVF] = (np.float32(10000.0) ** (-np.arange(128, dtype=np.float32) / np.float32(128))).astype(np.float64)
    return c.astype(np.float32)


_CACHE = {}


def _run(inp, S, layers=(0, 1, 2, 3), mlp=True, ncores=8, mix=True):
    key = (S, tuple(layers), mlp, mix)
    if key not in _CACHE:
        _CACHE[key] = Builder(S, layers, mlp, mix).build()
    nc = _CACHE[key]
    wblk = _pack_weights(inp)
    vecs = _pack_vecs(inp)
    cst = _consts()
    x = np.asarray(inp["x"], np.float32)
    pos = np.asarray(inp["positions"], np.int32)
    in_maps = []
    for b in range(ncores):
        in_maps.append({"xT": np.ascontiguousarray(x[b, :S].T), "wblk": wblk, "vecs": vecs, "cst": cst,
                        "pos": np.ascontiguousarray(pos[b, :S].reshape(1, S))})
    res = run_bass_kernel_spmd(nc, in_maps, core_ids=list(range(ncores)))
    out = np.stack([np.ascontiguousarray(res.results[b]["outT"].T) for b in range(ncores)], 0)
    return out.astype(np.float32)


def kernel(**inputs):
    inp = {k: np.asarray(v) for k, v in inputs.items()}
    return _run(inp, 4096)
```
